# Optimizing a Trainium2 kernel written in Bass

```python
import jax, jax.numpy as jnp
from jax import lax
import numpy as np

D_MODEL = 2048
BATCH = 32
SEQ = 256
DEPTH = 4
DEC_BATCH = 4
DEC_SEQ = 2048
PAST_LEN = 256

GRID_W = 64
WIN_H = 8
WIN_W = 16
Q_BLK_W = 16
K_BLK_W = 32
N_COL_BLK = GRID_W // Q_BLK_W
ATT_WIDTH = D_MODEL // 2
N_HEADS = 8
HEAD_DIM = ATT_WIDTH // N_HEADS
CONV_WIDTH = D_MODEL // 4
CONV_K = 31
POOL_WIDTH = D_MODEL - ATT_WIDTH - CONV_WIDTH
POOL_WINDOWS = (2, 4, 8, 16)
N_POOL_GROUPS = len(POOL_WINDOWS)
POOL_GROUP = POOL_WIDTH // N_POOL_GROUPS
D_IN = 3 * ATT_WIDTH + 2 * CONV_WIDTH + POOL_WIDTH
D_FF = -(-8 * D_MODEL // (3 * 256)) * 256
N_MOD = 6
Q_BLOCK = 128
EPS = 1e-6
NEG = -1e30

kernel_name = 'hybrid_natten_conformer_pool_diffusion_step'


def rmsnorm(x, g):
    xf = x.astype(jnp.float32)
    y = xf * lax.rsqrt(jnp.mean(xf * xf, axis=-1, keepdims=True) + EPS)
    return (y * g.astype(jnp.float32)).astype(x.dtype)


def layernorm(x, g, b):
    xf = x.astype(jnp.float32)
    mu = jnp.mean(xf, axis=-1, keepdims=True)
    var = jnp.mean(jnp.square(xf - mu), axis=-1, keepdims=True)
    y = (xf - mu) * lax.rsqrt(var + EPS)
    return (y * g.astype(jnp.float32) + b.astype(jnp.float32)).astype(x.dtype)


def modulation(cv, w_ada, b_ada):
    m = (jax.nn.silu(cv) @ w_ada + b_ada)[:, None, :]
    return jnp.split(m, N_MOD, axis=-1)


def dense_attention(q, k, v):
    B, T, H, Dh = q.shape
    qb = (q * Dh ** -0.5).reshape(B, T // Q_BLOCK, Q_BLOCK, H, Dh).transpose(1, 0, 2, 3, 4)

    def blk(qi):
        s = jnp.einsum('bqhd,bkhd->bhqk', qi, k).astype(jnp.float32)
        p = jax.nn.softmax(s, axis=-1).astype(v.dtype)
        return jnp.einsum('bhqk,bkhd->bqhd', p, v)

    o = lax.map(blk, qb)
    return o.transpose(1, 0, 2, 3, 4).reshape(B, T, H * Dh)


def neighbourhood_attention(q, k, v, k_ctx, v_ctx, rpb):
    B, N, H, Dh = q.shape
    rows = N // GRID_W
    kh = min(WIN_H, rows)
    qg = (q * Dh ** -0.5).reshape(B, rows, N_COL_BLK, Q_BLK_W, H, Dh).transpose(1, 0, 2, 3, 4, 5)
    kg = k.reshape(B, rows, GRID_W, H, Dh)
    vg = v.reshape(B, rows, GRID_W, H, Dh)
    qcol = jnp.arange(GRID_W).reshape(N_COL_BLK, Q_BLK_W)
    kc0 = jnp.clip(jnp.arange(N_COL_BLK) * Q_BLK_W - WIN_W // 2, 0, GRID_W - K_BLK_W)
    kcol = kc0[:, None] + jnp.arange(K_BLK_W)
    qstart = jnp.clip(qcol - WIN_W // 2, 0, GRID_W - WIN_W)
    kc = kcol[:, None, :]
    col_valid = (kc >= qstart[..., None]) & (kc < qstart[..., None] + WIN_W)
    col_idx = jnp.clip(kc - qcol[..., None], -(WIN_W - 1), WIN_W - 1) + WIN_W - 1
    nwin = kh * K_BLK_W

    def row_fn(args):
        r, q_r = args
        sr = jnp.clip(r - kh // 2, 0, rows - kh)
        k_win = lax.dynamic_slice_in_dim(kg, sr, kh, axis=1)[:, :, kcol]
        v_win = lax.dynamic_slice_in_dim(vg, sr, kh, axis=1)[:, :, kcol]
        s_win = jnp.einsum('bnqhd,bjnmhd->bhnqjm', q_r, k_win).astype(jnp.float32)
        row_idx = sr + jnp.arange(kh) - r + WIN_H - 1
        bias = rpb[:, row_idx][:, :, col_idx].transpose(0, 2, 3, 1, 4).astype(jnp.float32)
        s_win = jnp.where(col_valid[None, None, :, :, None, :], s_win + bias[None], NEG)
        s_ctx = jnp.einsum('bnqhd,blhd->bhnql', q_r, k_ctx).astype(jnp.float32)
        s = jnp.concatenate([s_win.reshape(B, H, N_COL_BLK, Q_BLK_W, nwin), s_ctx], axis=-1)
        p = jax.nn.softmax(s, axis=-1).astype(v.dtype)
        p_win = p[..., :nwin].reshape(B, H, N_COL_BLK, Q_BLK_W, kh, K_BLK_W)
        p_ctx = p[..., nwin:]
        o = (jnp.einsum('bhnqjm,bjnmhd->bnqhd', p_win, v_win)
             + jnp.einsum('bhnql,blhd->bnqhd', p_ctx, v_ctx))
        return o.reshape(B, GRID_W, H * Dh)

    o = lax.map(row_fn, (jnp.arange(rows), qg))
    return o.transpose(1, 0, 2, 3).reshape(B, N, H * Dh)


def conv_module(u, w_dw, b_dw, ln_g, ln_b, w_pw, b_pw):
    a, g = jnp.split(u, 2, axis=-1)
    h = a * jax.nn.sigmoid(g)
    h = lax.conv_general_dilated(h, w_dw, window_strides=(1,), padding=[(CONV_K // 2, CONV_K // 2)],
                                 dimension_numbers=('NWC', 'WIO', 'NWC'),
                                 feature_group_count=CONV_WIDTH) + b_dw
    h = jax.nn.silu(layernorm(h, ln_g, ln_b))
    return h @ w_pw + b_pw


def pool_mixer(u, w_pool, pool_scale):
    B, T, C = u.shape
    uf = u.astype(jnp.float32)
    cs = jnp.concatenate([jnp.zeros((B, 1, C), jnp.float32), jnp.cumsum(uf, axis=1)], axis=1)
    t = jnp.arange(T)
    outs = []
    for gi, w in enumerate(POOL_WINDOWS):
        lo = jnp.clip(t - w // 2, 0, T)
        hi = jnp.clip(t - w // 2 + w, 0, T)
        csg = cs[..., gi * POOL_GROUP:(gi + 1) * POOL_GROUP]
        mean = (csg[:, hi] - csg[:, lo]) / (hi - lo).astype(jnp.float32)[None, :, None]
        outs.append(mean - uf[..., gi * POOL_GROUP:(gi + 1) * POOL_GROUP])
    d = jnp.stack(outs, axis=2).astype(u.dtype)
    y = jnp.einsum('btgi,gio->btgo', d, w_pool).reshape(B, T, C)
    return y * pool_scale


def trunk_layer(x, mod, attend, g_pre_mix, g_post_mix, g_pre_ffn, g_post_ffn, w_in,
                w_dw, b_dw, ln_g, ln_b, w_pw, b_pw, w_pool, pool_scale, w_out, w_ffn_in, w_ffn_out):
    sh1, sc1, g1, sh2, sc2, g2 = mod
    B, T, _ = x.shape
    h = rmsnorm(x, g_pre_mix) * (1 + sc1) + sh1
    z = h @ w_in
    q, k, v, u_conv, u_pool = jnp.split(
        z, [ATT_WIDTH, 2 * ATT_WIDTH, 3 * ATT_WIDTH, 3 * ATT_WIDTH + 2 * CONV_WIDTH], axis=-1)
    q = q.reshape(B, T, N_HEADS, HEAD_DIM)
    k = k.reshape(B, T, N_HEADS, HEAD_DIM)
    v = v.reshape(B, T, N_HEADS, HEAD_DIM)
    a = attend(q, k, v)
    cm = conv_module(u_conv, w_dw, b_dw, ln_g, ln_b, w_pw, b_pw)
    pm = pool_mixer(u_pool, w_pool, pool_scale)
    o = jnp.concatenate([a, cm, pm], axis=-1) @ w_out
    x = x + g1 * rmsnorm(o, g_post_mix)
    h = rmsnorm(x, g_pre_ffn) * (1 + sc2) + sh2
    gt, up = jnp.split(h @ w_ffn_in, 2, axis=-1)
    f = (jax.nn.silu(gt) * up) @ w_ffn_out
    x = x + g2 * rmsnorm(f, g_post_ffn)
    return x, k, v


def setup_inputs(seed: int = 0) -> dict:
    key = jax.random.key(seed)
    ks = jax.random.split(key, 32)
    f32 = jnp.float32
    nrm = lambda k, shape, s: jax.random.normal(k, shape, f32) * s
    gain = lambda k, shape: 1.0 + 0.02 * jax.random.normal(k, shape, f32)
    return {
        'x_prompt': nrm(ks[0], (BATCH, SEQ, D_MODEL), 1.0),
        'x_sample': nrm(ks[1], (DEC_BATCH, DEC_SEQ, D_MODEL), 1.0),
        'cache_k': nrm(ks[2], (DEC_BATCH, DEPTH, PAST_LEN, N_HEADS, HEAD_DIM), 1.0),
        'cache_v': nrm(ks[3], (DEC_BATCH, DEPTH, PAST_LEN, N_HEADS, HEAD_DIM), 1.0),
        'c': nrm(ks[4], (DEC_BATCH, D_MODEL), 1.0),
        'c_ctx': nrm(ks[5], (D_MODEL,), 1.0),
        'w_ada': nrm(ks[6], (DEPTH, D_MODEL, N_MOD * D_MODEL), 0.5 * D_MODEL ** -0.5),
        'b_ada': nrm(ks[7], (DEPTH, N_MOD * D_MODEL), 0.01),
        'g_pre_mix': gain(ks[8], (DEPTH, D_MODEL)),
        'g_post_mix': gain(ks[9], (DEPTH, D_MODEL)),
        'g_pre_ffn': gain(ks[10], (DEPTH, D_MODEL)),
        'g_post_ffn': gain(ks[11], (DEPTH, D_MODEL)),
        'w_in': nrm(ks[12], (DEPTH, D_MODEL, D_IN), D_MODEL ** -0.5),
        'rpb': nrm(ks[13], (DEPTH, N_HEADS, 2 * WIN_H - 1, 2 * WIN_W - 1), 0.1),
        'w_dw': nrm(ks[14], (DEPTH, CONV_K, 1, CONV_WIDTH), CONV_K ** -0.5),
        'b_dw': nrm(ks[15], (DEPTH, CONV_WIDTH), 0.01),
        'ln_conv_g': gain(ks[16], (DEPTH, CONV_WIDTH)),
        'ln_conv_b': nrm(ks[17], (DEPTH, CONV_WIDTH), 0.01),
        'w_pw': nrm(ks[18], (DEPTH, CONV_WIDTH, CONV_WIDTH), CONV_WIDTH ** -0.5),
        'b_pw': nrm(ks[19], (DEPTH, CONV_WIDTH), 0.01),
        'w_pool': nrm(ks[20], (DEPTH, N_POOL_GROUPS, POOL_GROUP, POOL_GROUP), POOL_GROUP ** -0.5),
        'pool_scale': gain(ks[21], (DEPTH, POOL_WIDTH)),
        'w_out': nrm(ks[22], (DEPTH, D_MODEL, D_MODEL), D_MODEL ** -0.5),
        'w_ffn_in': nrm(ks[23], (DEPTH, D_MODEL, 2 * D_FF), D_MODEL ** -0.5),
        'w_ffn_out': nrm(ks[24], (DEPTH, D_FF, D_MODEL), D_FF ** -0.5),
    }


def reference(x_prompt, x_sample, cache_k, cache_v, c, c_ctx, w_ada, b_ada, g_pre_mix, g_post_mix,
              g_pre_ffn, g_post_ffn, w_in, rpb, w_dw, b_dw, ln_conv_g, ln_conv_b, w_pw, b_pw,
              w_pool, pool_scale, w_out, w_ffn_in, w_ffn_out):
    def params(l):
        return (g_pre_mix[l], g_post_mix[l], g_pre_ffn[l], g_post_ffn[l], w_in[l],
                w_dw[l], b_dw[l], ln_conv_g[l], ln_conv_b[l], w_pw[l], b_pw[l],
                w_pool[l], pool_scale[l], w_out[l], w_ffn_in[l], w_ffn_out[l])

    xp = x_prompt
    ks_new = []
    vs_new = []
    for l in range(DEPTH):
        mod = modulation(c_ctx[None, :], w_ada[l], b_ada[l])
        xp, k_l, v_l = trunk_layer(xp, mod, dense_attention, *params(l))
        ks_new.append(k_l)
        vs_new.append(v_l)
    new_cache_k = jnp.stack(ks_new, axis=1)
    new_cache_v = jnp.stack(vs_new, axis=1)

    xs = x_sample
    for l in range(DEPTH):
        mod = modulation(c, w_ada[l], b_ada[l])
        kc_l = cache_k[:, l]
        vc_l = cache_v[:, l]
        rpb_l = rpb[l]
        attend = lambda q, k, v, kc_l=kc_l, vc_l=vc_l, rpb_l=rpb_l: neighbourhood_attention(
            q, k, v, kc_l, vc_l, rpb_l)
        xs, _, _ = trunk_layer(xs, mod, attend, *params(l))

    return (xp, xs, new_cache_k, new_cache_v)
```

```python
import contextlib
import numpy as np
import concourse.bass as bass
import concourse.mybir as mybir
from concourse.bass_utils import run_bass_kernel_spmd

F32 = mybir.dt.float32
BF16 = mybir.dt.bfloat16
AF = mybir.ActivationFunctionType
ALU = mybir.AluOpType

D = 2048
T = 2048
KC = 16
NL = 4
DIN = 4608
DFF = 5632
KCF = 44
NH = 8
EPS = 1e-6
NEGM = -30000.0
NPAT = 6
ARENA_BYTES = 100 * 1024
ENGS = ["tensor", "vector", "scalar", "gpsimd", "sync"]


def pat_of(i):
    if i == 0:
        return 0
    if i == 1:
        return 1
    if i == 14:
        return 4
    if i == 15:
        return 5
    return 2 if i % 2 == 0 else 3


def wb0_of(i):
    return min(max(i - 2, 0), 11)


class Op:
    __slots__ = ("eng", "fn", "deps", "idx", "needs_inc", "dma_key", "inc_val")

    def __init__(self, eng, fn, deps, idx, dma_key):
        self.eng = eng
        self.fn = fn
        self.deps = deps
        self.idx = idx
        self.needs_inc = False
        self.dma_key = dma_key
        self.inc_val = 0


class Prog:
    def __init__(self):
        self.ops = {e: [] for e in ENGS}
        self.res = {}
        self.dma_cnt = {}

    def _res(self, name):
        r = self.res.get(name)
        if r is None:
            r = {"w": {}, "r": {}}
            self.res[name] = r
        return r

    @staticmethod
    def _merge(dst, src):
        for k, v in src.items():
            if dst.get(k, -1) < v:
                dst[k] = v

    def op(self, eng, fn, r=(), w=(), dma_key=None):
        r = list(r)
        w = list(w)
        if any(n.startswith("@") for n in r + w):
            r.append("ARENA")
        deps = {}
        for n in r:
            self._merge(deps, self._res(n)["w"])
        for n in w:
            rr = self._res(n)
            self._merge(deps, rr["w"])
            self._merge(deps, rr["r"])
        idx = len(self.ops[eng])
        if eng == "tensor":
            deps.pop(("c", "tensor"), None)
        o = Op(eng, fn, deps, idx, dma_key)
        self.ops[eng].append(o)
        for k, v in deps.items():
            if k[0] == "c":
                self.ops[k[1]][v].needs_inc = True
        if dma_key is not None:
            self.dma_cnt[dma_key] = self.dma_cnt.get(dma_key, 0) + 16
            tok = {("d", dma_key): self.dma_cnt[dma_key]}
        else:
            tok = {("c", eng): idx}
        for n in r:
            self._merge(self._res(n)["r"], tok)
        for n in w:
            rr = self._res(n)
            rr["w"] = dict(tok)
            rr["r"] = {}
        return o

    def dma(self, queue, out, in_, key, r=(), w=()):
        def fn(e, out=out, in_=in_):
            return e.dma_start(out=out, in_=in_)
        return self.op(queue, fn, r, w, dma_key=key)

    def emit(self, nc, es):
        csem = {e: es.enter_context(nc.semaphore("c_" + e)) for e in ENGS}
        dsem = {k: es.enter_context(nc.semaphore("d_%d" % i)) for i, k in enumerate(sorted(self.dma_cnt))}
        for e in ENGS:
            c = 0
            for o in self.ops[e]:
                if o.needs_inc and o.dma_key is None:
                    c += 1
                o.inc_val = c
        block = es.enter_context(nc.Block())
        prog = self

        def run(eng_name, eobj):
            waited = {}
            for o in prog.ops[eng_name]:
                for k, v in o.deps.items():
                    if k[0] == "c":
                        sem = csem[k[1]]
                        val = prog.ops[k[1]][v].inc_val
                    else:
                        sem = dsem[k[1]]
                        val = v
                    if waited.get(k, 0) < val:
                        eobj.wait_ge(sem, val)
                        waited[k] = val
                ins = o.fn(eobj)
                if o.dma_key is not None:
                    ins.then_inc(dsem[o.dma_key], 16)
                elif o.needs_inc:
                    ins.then_inc(csem[eng_name], 1)
            if eng_name == "sync":
                for k, v in prog.dma_cnt.items():
                    eobj.wait_ge(dsem[k], v)

        @block.tensor
        def _(t):
            run("tensor", t)

        @block.vector
        def _(v):
            run("vector", v)

        @block.scalar
        def _(s):
            run("scalar", s)

        @block.gpsimd
        def _(g):
            run("gpsimd", g)

        @block.sync
        def _(sy):
            run("sync", sy)


def build_program(n_layers=NL):
    nc = bass.Bass("TRN2", target_bir_lowering=False)

    def din(name, shape):
        return nc.dram_tensor(name, list(shape), F32, kind="ExternalInput").ap()

    xT_in = din("xT_in", [KC, 128, T])
    cvec = din("cvec", [128, KC])
    w_ada = din("w_ada", [NL, D, 6 * D])
    b_adaT = din("b_adaT", [128, NL, 96])
    gains = din("gains", [128, NL, 4, KC])
    w_in = din("w_in", [NL, D, DIN])
    w_out = din("w_out", [NL, D, D])
    w_ffn_in = din("w_ffn_in", [NL, D, 2 * DFF])
    w_ffn_out = din("w_ffn_out", [NL, DFF, D])
    w_pw = din("w_pw", [NL, 512, 512])
    w_pool = din("w_pool", [NL, 4, 128, 128])
    wdw = din("wdw", [128, NL, 4, 31])
    cpar = din("cpar", [128, NL, 6, 4])
    ident_in = din("ident", [128, 128])
    abias = din("abias", [NL, NH, 128, NPAT, 640])
    kctxT = din("kctxT", [NL, NH, 128, 256])
    vctx = din("vctx", [NL, NH, 128, 2, 128])
    cb_in = din("cb", [128, 1])
    flag_in = din("flag", [128, 1])
    invcnt = din("invcnt", [4, 128, T])

    yT = nc.dram_tensor("yT", [KC, 128, T], F32, kind="ExternalOutput").ap()
    kT_out = nc.dram_tensor("kT_out", [NL, NH, 128, T], F32, kind="ExternalOutput").ap()
    vT_out = nc.dram_tensor("vT_out", [NL, NH, 128, T], F32, kind="ExternalOutput").ap()
    sT_d = nc.dram_tensor("sT_d", [KC, 128, T], F32, kind="Internal").ap()
    cat_d = nc.dram_tensor("cat_d", [KC, 128, T], BF16, kind="Internal").ap()

    P = Prog()
    es = contextlib.ExitStack()

    def sb(name, shape, dt):
        return es.enter_context(nc.sbuf_tensor(name, list(shape), dt))

    hT = sb("hT", [128, KC, T], BF16)
    wsl = [sb("wsl%d" % i, [128, 4096], BF16) for i in range(4)]
    arena = sb("arena", [128, ARENA_BYTES // 2], BF16)
    ones_bf = sb("ones_bf", [128, 128], BF16)
    ones_f = sb("ones_f", [128, 128], F32)
    ident = sb("ident_bf", [128, 128], BF16)
    modT = sb("modT", [128, 96], F32)
    vec = sb("vec", [128, 2, 6, KC], F32)
    gains_sb = sb("gains_sb", [128, NL, 4, KC], F32)
    badaT_sb = sb("badaT_sb", [128, 96], F32)
    cvec_sb = sb("cvec_sb", [128, KC], F32)
    siluc = sb("siluc", [128, KC], F32)
    siluc_rep = sb("siluc_rep", [128, KC, 128], BF16)
    mtmp = sb("mtmp", [128, 2, 128], F32)
    cbt = sb("cbt", [128, 256], BF16)
    cpar_sb = sb("cpar_sb", [128, NL, 6, 4], F32)
    cb_sb = sb("cb_sb", [128, 1], F32)
    flag_sb = sb("flag_sb", [128, 1], F32)
    eps_sb = sb("eps_sb", [128, 1], F32)
    dummy = sb("dummy_t", [128, 8], F32)
    psb = [es.enter_context(nc.psum_tensor("psb%d" % i, [128, 512], F32)) for i in range(8)]

    def carve(off, shape, dt):
        n = int(np.prod(shape[1:]))
        nb = n * (4 if dt == F32 else 2)
        assert off % 4 == 0 and off + nb <= ARENA_BYTES, (off, nb)
        a = arena[:, off // 2: (off + nb) // 2]
        if dt == F32:
            a = a.bitcast(F32)
        if len(shape) == 3:
            a = a.rearrange("p (a b) -> p a b", a=shape[1])
        elif len(shape) == 4:
            a = a.rearrange("p (a b c) -> p a b c", a=shape[1], b=shape[2])
        return a

    def barrier():
        P.op("vector", lambda v: v.memset(dummy[:], 0.0), r=[], w=["ARENA"])

    def PE(mms, r, w):
        def fn(t, mms=mms):
            last = None
            for (o, l, rh, st, sp) in mms:
                last = t.matmul(o, lhsT=l, rhs=rh, start=st, stop=sp)
            return last
        return P.op("tensor", fn, r, w)

    def ACT(out, in_, func, r, w, bias=None, scale=None):
        def fn(s, out=out, in_=in_, func=func, bias=bias, scale=scale):
            kw = {}
            if bias is not None:
                kw["bias"] = bias
            if scale is not None:
                kw["scale"] = scale
            return s.activation(out=out, in_=in_, func=func, **kw)
        return P.op("scalar", fn, r, w)

    def DVE(fn, r, w):
        return P.op("vector", fn, r, w)

    wstate = {"n": 0}

    def load_w(src2d, kcn, ncols, rows0=0):
        s = wstate["n"] % 4
        wstate["n"] += 1
        view = wsl[s][:, 0: kcn * ncols].rearrange("p (k n) -> p k n", k=kcn)
        src = src2d[rows0: rows0 + kcn * 128, :].rearrange("(kc p) n -> p kc n", p=128)
        P.dma("gpsimd", view, src, "w%d" % s, r=[], w=["w%d" % s])
        return view, "w%d" % s

    pair_state = {"n": 0}

    PAIRS3 = [(0, 1), (2, 3), (6, 7)]

    def next_pair(npairs=2):
        pair_state["n"] += 1
        if npairs == 3:
            return PAIRS3[pair_state["n"] % 3]
        p = pair_state["n"] % 2
        return (2 * p, 2 * p + 1)

    def mm_half(wv, wres, n_off, kcn, rhs_of, act_res, pair, first=True, last=True):
        mms = []
        for kc in range(kcn):
            for tt in range(2):
                mms.append((psb[pair[tt]][:], wv[:, kc, n_off: n_off + 128], rhs_of(kc, tt),
                            first and kc == 0, last and kc == kcn - 1))
        PE(mms, r=[wres] + list(act_res), w=["ps%d" % pair[0], "ps%d" % pair[1]])

    def act_rhs(half):
        def f(kc, tt, half=half):
            t0 = half * 1024 + tt * 512
            return hT[:, kc, t0: t0 + 512]
        return f

    P.dma("sync", gains_sb[:], gains, "ld_gains", w=["gains"])
    P.dma("sync", cvec_sb[:], cvec, "ld_cvec", w=["cvec"])
    P.dma("sync", cpar_sb[:], cpar, "ld_cpar", w=["cpar"])
    P.dma("sync", cb_sb[:], cb_in, "ld_cb", w=["cb"])
    P.dma("sync", flag_sb[:], flag_in, "ld_flag", w=["flag"])
    P.dma("gpsimd", ident[:], ident_in, "ld_ident", w=["ident"])
    DVE(lambda v: v.memset(ones_bf[:], 1.0), r=[], w=["ones_bf"])
    DVE(lambda v: v.memset(ones_f[:], 1.0), r=[], w=["ones_f"])
    DVE(lambda v: v.memset(eps_sb[:], EPS), r=[], w=["eps"])
    ACT(siluc[:], cvec_sb[:], AF.Silu, r=["cvec"], w=["siluc"])
    DVE(lambda v: v.memset(cbt[:], 1.0), r=[], w=["cbt"])
    DVE(lambda v: v.tensor_scalar(out=cbt[:], in0=cbt[:], scalar1=cb_sb[:], scalar2=None, op0=ALU.mult),
        r=["cbt", "cb"], w=["cbt"])
    DVE(lambda v: v.tensor_copy(out=siluc_rep[:], in_=siluc[:].unsqueeze(2).to_broadcast([128, KC, 128])),
        r=["siluc"], w=["siluc_rep"])

    def mod_epilogue(l, c0, c1, rows):
        s = l % 2
        vr = "vec%d" % s
        if c0 == 0:
            P.dma("sync", badaT_sb[:], b_adaT[:, l, :], "ld_bada", w=["bada"])
        DVE(lambda v: v.tensor_tensor(out=modT[:, c0:c1], in0=modT[:, c0:c1], in1=badaT_sb[:, c0:c1], op=ALU.add),
            r=["bada", "modT"], w=["modT"])
        if 0 in rows:
            DVE(lambda v: v.scalar_tensor_tensor(out=vec[:, s, 0, :], in0=modT[:, 16:32], scalar=1.0,
                                                 in1=gains_sb[:, l, 0, :], op0=ALU.add, op1=ALU.mult),
                r=["modT", "gains"], w=[vr])
        if 1 in rows:
            DVE(lambda v: v.tensor_copy(out=vec[:, s, 1, :], in_=modT[:, 0:16]), r=["modT"], w=[vr])
        if 2 in rows:
            DVE(lambda v: v.tensor_tensor(out=vec[:, s, 2, :], in0=modT[:, 32:48], in1=gains_sb[:, l, 1, :],
                                          op=ALU.mult), r=["modT", "gains"], w=[vr])
        if 3 in rows:
            DVE(lambda v: v.scalar_tensor_tensor(out=vec[:, s, 3, :], in0=modT[:, 64:80], scalar=1.0,
                                                 in1=gains_sb[:, l, 2, :], op0=ALU.add, op1=ALU.mult),
                r=["modT", "gains"], w=[vr])
        if 4 in rows:
            DVE(lambda v: v.tensor_copy(out=vec[:, s, 4, :], in_=modT[:, 48:64]), r=["modT"], w=[vr])
        if 5 in rows:
            DVE(lambda v: v.tensor_tensor(out=vec[:, s, 5, :], in0=modT[:, 80:96], in1=gains_sb[:, l, 3, :],
                                          op=ALU.mult), r=["modT", "gains"], w=[vr])

    def modulation(l, split=False):
        for wb in range(48):
            wv, wres = load_w(w_ada[l][:, wb * 256: (wb + 1) * 256], KC, 256)
            PE([(psb[7][:, 0:256], siluc_rep[:, kc, :], wv[:, kc, :], kc == 0, kc == KC - 1) for kc in range(KC)],
               r=[wres, "siluc_rep"], w=["ps7"])
            DVE(lambda v: v.tensor_tensor(out=mtmp[:], in0=psb[7][:, 0:256].rearrange("p (a b) -> p a b", a=2),
                                          in1=ident[:].unsqueeze(1).to_broadcast([128, 2, 128]), op=ALU.mult),
                r=["ps7", "ident"], w=["mtmp"])
            DVE(lambda v, wb=wb: v.tensor_reduce(out=modT[:, 2 * wb: 2 * wb + 2], in_=mtmp[:], axis=mybir.AxisListType.X,
                                                 op=ALU.add), r=["mtmp"], w=["modT"])
            if split and wb == 15:
                mod_epilogue(l, 0, 32, (0, 1))
            yield
        if split:
            mod_epilogue(l, 32, 96, (2, 3, 4, 5))
        else:
            mod_epilogue(l, 0, 96, (0, 1, 2, 3, 4, 5))

    def xres(oc, half):
        return "xd%d_%d" % (oc, half)

    def prenorm(l, sub, tts, xsrc):
        s = l % 2
        vr = "vec%d" % s
        ia, ib = (0, 1) if sub == 0 else (3, 4)
        xt = [carve(0, [128, KC, 512], F32), carve(32768, [128, KC, 512], F32)]
        sq = carve(65536, [128, KC, 512], BF16)
        rs = [carve(81920, [128, 512], F32), carve(83968, [128, 512], F32)]
        rstd = [carve(86016, [128, 512], F32), carve(88064, [128, 512], F32)]

        def stage_a(n):
            tt = tts[n]
            k = n % 2
            half = tt // 2
            xr = "@xt%d" % k
            P.dma("sync", xt[k], xsrc[:, :, tt * 512: (tt + 1) * 512].rearrange("k p n -> p k n"), "xt%d" % k,
                  r=[xres(oc, half) for oc in range(KC)], w=[xr])
            ACT(sq[:, 0:8, :], xt[k][:, 0:8, :], AF.Square, r=[xr], w=["@sqa"])
            P.op("gpsimd", lambda g, k=k: g.tensor_tensor(out=sq[:, 8:16, :], in0=xt[k][:, 8:16, :], in1=xt[k][:, 8:16, :],
                                                         op=ALU.mult), r=[xr], w=["@sqb"])
            bank = 6 + k
            PE([(psb[bank][:], ones_bf[:], sq[:, kc, :], kc == 0, kc == KC - 1) for kc in range(KC)],
               r=["@sqa", "@sqb", "ones_bf"], w=["ps%d" % bank])
            ACT(rs[k], psb[bank][:], AF.Sqrt, r=["ps%d" % bank, "eps"], w=["@rs%d" % k], bias=eps_sb[:], scale=1.0 / D)

        def stage_b(n):
            k = n % 2
            xr = "@xt%d" % k
            DVE(lambda v, k=k: v.reciprocal(out=rstd[k], in_=rs[k]), r=["@rs%d" % k], w=["@rstd%d" % k])
            DVE(lambda v, k=k: v.tensor_tensor(out=xt[k], in0=xt[k],
                                               in1=rstd[k].unsqueeze(1).to_broadcast([128, KC, 512]), op=ALU.mult),
                r=[xr, "@rstd%d" % k], w=[xr])

        def stage_c(n):
            tt = tts[n]
            k = n % 2
            half = tt // 2
            xr = "@xt%d" % k
            for kc in range(KC):
                ACT(hT[:, kc, tt * 512: (tt + 1) * 512], xt[k][:, kc, :], AF.Identity, r=[xr, vr],
                    w=["act%d" % half], bias=vec[:, s, ib, kc: kc + 1], scale=vec[:, s, ia, kc: kc + 1])

        nt = len(tts)
        stage_a(0)
        for n in range(nt):
            if n + 1 < nt:
                stage_a(n + 1)
            stage_b(n)
            stage_c(n)

    def linear_post(l, sub, half, kcn_total, wsrc, rhs_of, act_res, xsrc):
        s = l % 2
        vr = "vec%d" % s
        ig = 2 if sub == 0 else 5
        base = 90112
        stg = [carve(base, [128, 1024], F32), carve(base + 4096, [128, 1024], F32)]
        sqo = carve(base + 8192, [128, 1024], BF16)
        b6, b7 = 4, 5
        t0 = half * 1024
        keep = (sub == 0)
        obuf = carve(0, [128, KC, 1024], F32) if keep else None
        cur = None
        for oc in range(KC):
            pair = next_pair(3)
            c0 = (oc // 2) * 256
            if kcn_total == KC:
                if oc % 2 == 0:
                    cur = [load_w(wsrc[:, c0: c0 + 256], KC, 256) + (0, KC)]
            else:
                if oc % 2 == 0:
                    cur = [load_w(wsrc[:, c0: c0 + 256], kn, 256, rows0=r0) + (r0 // 128, kn)
                           for (r0, kn) in ((0, 16), (2048, 16), (4096, 12))]
            for g, (wv, wres, kc0, kn) in enumerate(cur):
                mm_half(wv, wres, (oc % 2) * 128, kn,
                        (lambda kc, tt, kc0=kc0: rhs_of(kc0 + kc, tt)), act_res, pair,
                        first=(g == 0), last=(g == len(cur) - 1))
            k = oc % 2
            if keep:
                dst, sr = obuf[:, oc, :], "@ob%d" % oc
            else:
                dst, sr = stg[k], "@stg%d" % k
            for tt in range(2):
                ACT(dst[:, tt * 512: (tt + 1) * 512], psb[pair[tt]][:], AF.Identity,
                    r=["ps%d" % pair[tt]], w=[sr])
            if not keep:
                P.dma("sync", sT_d[oc][:, t0: t0 + 1024], stg[k], "stg%d" % k, r=[sr], w=["sd%d_%d" % (oc, half)])
            ACT(sqo, dst, AF.Square, r=[sr], w=["@sqo"])
            PE([(psb[b6 + tt][:], ones_bf[:], sqo[:, tt * 512: (tt + 1) * 512], oc == 0, oc == KC - 1)
                for tt in range(2)], r=["@sqo", "ones_bf"], w=["ps%d" % b6, "ps%d" % b7])
        if keep:
            rso, rsr = carve(81920 + half * 4096, [128, 1024], F32), "@rsp%d" % half
        else:
            rso, rsr = carve(base + 8192, [128, 1024], F32), "@sqo"
        for tt in range(2):
            ACT(rso[:, tt * 512: (tt + 1) * 512], psb[b6 + tt][:], AF.Sqrt, r=["ps%d" % (b6 + tt), rsr],
                w=[rsr], bias=eps_sb[:], scale=1.0 / D)
        DVE(lambda v: v.reciprocal(out=rso, in_=rso), r=[rsr], w=[rsr])
        if not keep:
            barrier()
        if keep:
            xb = [carve(65536, [128, 2, 1024], F32), carve(73728, [128, 2, 1024], F32), carve(90112, [128, 2, 1024], F32)]
            fb = None
        else:
            xb = [carve(32768 + i * 8192, [128, 2, 1024], F32) for i in range(4)]
            fb = [carve(i * 8192, [128, 2, 1024], F32) for i in range(4)]
        nb = len(xb)

        def loads(g):
            k = g % nb
            P.dma("sync", xb[k], xsrc[2 * g: 2 * g + 2, :, t0: t0 + 1024].rearrange("k p n -> p k n"), "xb%d" % k,
                  r=[xres(2 * g, half), xres(2 * g + 1, half)], w=["@xb%d" % k])
            if not keep:
                P.dma("sync", fb[k], sT_d[2 * g: 2 * g + 2, :, t0: t0 + 1024].rearrange("k p n -> p k n"), "fb%d" % k,
                      r=["sd%d_%d" % (2 * g, half), "sd%d_%d" % (2 * g + 1, half)], w=["@fb%d_0" % k, "@fb%d_1" % k])

        for g in range(nb - 1):
            loads(g)
        for g in range(8):
            k = g % nb
            if g + nb - 1 < 8:
                loads(g + nb - 1)
            if keep:
                src = obuf[:, 2 * g: 2 * g + 2, :]
                srs = ["@ob%d" % (2 * g), "@ob%d" % (2 * g + 1)]
            else:
                src = fb[k]
                srs = ["@fb%d_0" % k, "@fb%d_1" % k]
            for n in range(2):
                oc = 2 * g + n
                DVE(lambda v, src=src, n=n, oc=oc: v.scalar_tensor_tensor(
                    out=src[:, n, :], in0=src[:, n, :], scalar=vec[:, s, ig, oc: oc + 1], in1=rso,
                    op0=ALU.mult, op1=ALU.mult), r=[srs[n], rsr, vr], w=[srs[n]])
            if g % 2 == 0:
                P.op("gpsimd", lambda e, k=k, src=src: e.tensor_tensor(out=xb[k], in0=xb[k], in1=src, op=ALU.add),
                     r=srs + ["@xb%d" % k], w=["@xb%d" % k])
            else:
                DVE(lambda v, k=k, src=src: v.tensor_tensor(out=xb[k], in0=xb[k], in1=src, op=ALU.add),
                    r=srs + ["@xb%d" % k], w=["@xb%d" % k])
            P.dma("sync" if keep else "scalar", yT[2 * g: 2 * g + 2, :, t0: t0 + 1024].rearrange("k p n -> p k n"), xb[k], "xbs%d" % k,
                  r=["@xb%d" % k], w=[xres(2 * g, half), xres(2 * g + 1, half)])

    def attention(l, modgens, mod_per_head):
        KT = carve(0, [128, 4, T], BF16)
        QT = carve(16384, [128, 4, T], BF16)
        Vtok = carve(32768, [128, 4, 16, 128], BF16)
        vTb = carve(49152, [128, T], BF16)
        aTs = [carve(53248, [128, T], BF16), carve(57344, [128, T], BF16)]
        stg = [carve(61440, [128, 1024], F32), carve(65536, [128, 1024], F32), carve(94720, [128, 1024], F32)]
        ab = [carve(69632, [128, NPAT, 640], BF16), carve(77312, [128, NPAT, 640], BF16)]
        kc_sb = carve(84992, [128, 4, 256], BF16)
        vc_sb = carve(87040, [128, 4, 2, 128], BF16)
        PT = [carve(89088, [128, 896], BF16), carve(90880, [128, 896], BF16)]
        rden = carve(92672, [128, 512], F32)
        scale = 1.0 / np.sqrt(128.0)
        stn = {"n": 0}
        abn = {"n": 0}
        ptn = {"n": 0}
        hcount = 0
        for g in range(2):
            P.dma("gpsimd", kc_sb, kctxT[l, 4 * g: 4 * g + 4].rearrange("h p n -> p h n"), "kcsb", w=["@kcsb"])
            P.dma("gpsimd", vc_sb, vctx[l, 4 * g: 4 * g + 4].rearrange("h p b d -> p h b d"), "vcsb", w=["@vcsb"])
            for which, c0 in (("k", 1024 + 512 * g), ("v", 2048 + 512 * g), ("q", 512 * g)):
                for blk in range(2):
                    wv, wres = load_w(w_in[l][:, c0 + blk * 256: c0 + blk * 256 + 256], KC, 256)
                    for n in range(2):
                        hl = blk * 2 + n
                        h = 4 * g + hl
                        for half in range(2):
                            pair = next_pair(2 if which == "v" else 3)
                            mm_half(wv, wres, n * 128, KC, act_rhs(half), ["act%d" % half], pair)
                            t0 = half * 1024
                            if which == "q":
                                for tt in range(2):
                                    ACT(QT[:, hl, t0 + tt * 512: t0 + tt * 512 + 512], psb[pair[tt]][:], AF.Copy,
                                        r=["ps%d" % pair[tt]], w=["@QT%d" % hl], scale=float(scale))
                            else:
                                k = stn["n"] % 3
                                stn["n"] += 1
                                sr = "@astg%d" % k
                                for tt in range(2):
                                    ACT(stg[k][:, tt * 512: (tt + 1) * 512], psb[pair[tt]][:], AF.Identity,
                                        r=["ps%d" % pair[tt]], w=[sr])
                                dst = (kT_out if which == "k" else vT_out)[l, h][:, t0: t0 + 1024]
                                P.dma("sync", dst, stg[k], "astg%d" % k, r=[sr], w=["kvout_%s_%d_%d" % (which, h, half)])
                                if which == "k":
                                    DVE(lambda v, k=k, hl=hl, t0=t0: v.tensor_copy(out=KT[:, hl, t0: t0 + 1024], in_=stg[k]),
                                        r=[sr], w=["@KT%d" % hl])
                                else:
                                    DVE(lambda v, k=k, t0=t0: v.tensor_copy(out=vTb[:, t0: t0 + 1024], in_=stg[k]),
                                        r=[sr], w=["@vTb"])
                                    pb = psb[6 + half]
                                    pbv = pb[:].bitcast(BF16)
                                    def tfn(t, t0=t0, pbv=pbv):
                                        last = None
                                        for b in range(8):
                                            last = t.transpose(pbv[:, b * 128: (b + 1) * 128],
                                                               vTb[:, t0 + b * 128: t0 + (b + 1) * 128], ident[:])
                                        return last
                                    P.op("tensor", tfn, r=["@vTb", "ident"], w=["ps%d" % (6 + half)])
                                    DVE(lambda v, hl=hl, half=half, pbv=pbv: v.tensor_copy(
                                        out=Vtok[:, hl, half * 8: half * 8 + 8, :].rearrange("p a b -> p (a b)"), in_=pbv),
                                        r=["ps%d" % (6 + half)], w=["@Vtok%d" % hl])
            for hl in range(4):
                h = 4 * g + hl
                ak = abn["n"] % 2
                abn["n"] += 1
                abr = "@ab%d" % ak
                P.dma("gpsimd", ab[ak], abias[l, h], "ab%d" % ak, w=[abr])
                aT = aTs[hcount % 2]
                aTr = "@aT%d" % (hcount % 2)
                hcount += 1
                def s_stage(i, hl=hl, ak=ak, abr=abr):
                    pid = pat_of(i)
                    wb0 = wb0_of(i)
                    pair = (0, 1) if i % 2 == 0 else (2, 3)
                    bA, bB = psb[pair[0]], psb[pair[1]]
                    q_ap = QT[:, hl, i * 128: (i + 1) * 128]
                    mms = [(bA[:, 0:512], ident[:], ab[ak][:, pid, 0:512], True, False)]
                    for j in range(4):
                        kb = wb0 + j
                        mms.append((bA[:, j * 128: (j + 1) * 128], KT[:, hl, kb * 128: (kb + 1) * 128], q_ap,
                                    False, j == 3))
                    mms.append((bB[:, 0:128], ident[:], ab[ak][:, pid, 512:640], True, False))
                    mms.append((bB[:, 128:384], ident[:], cbt[:], False, False))
                    kb = wb0 + 4
                    mms.append((bB[:, 0:128], KT[:, hl, kb * 128: (kb + 1) * 128], q_ap, False, False))
                    for cbk in range(2):
                        mms.append((bB[:, 128 + cbk * 128: 256 + cbk * 128], kc_sb[:, hl, cbk * 128: (cbk + 1) * 128],
                                    q_ap, False, cbk == 1))
                    PE(mms, r=[abr, "ident", "cbt", "@KT%d" % hl, "@QT%d" % hl, "@kcsb"],
                       w=["ps%d" % pair[0], "ps%d" % pair[1]])
                    pk = i % 2
                    ACT(PT[pk][:, 0:512], bA[:], AF.Exp, r=["ps%d" % pair[0]], w=["@PTa%d" % pk])
                    ACT(PT[pk][:, 512:896], bB[:, 0:384], AF.Exp, r=["ps%d" % pair[1]], w=["@PTb%d" % pk])

                def o_stage(i, hl=hl, aT=aT, aTr=aTr):
                    qg, ii = i // 4, i % 4
                    wb0 = wb0_of(i)
                    ob, db = (4, 5) if qg % 2 == 0 else (6, 7)
                    pk = i % 2
                    mms = []
                    for j in range(7):
                        mms.append((psb[db][:, ii * 128: (ii + 1) * 128], ones_bf[:], PT[pk][:, j * 128: (j + 1) * 128],
                                    j == 0, j == 6))
                    for j in range(7):
                        if j < 5:
                            vl = Vtok[:, hl, wb0 + j, :]
                        else:
                            vl = vc_sb[:, hl, j - 5, :]
                        mms.append((psb[ob][:, ii * 128: (ii + 1) * 128], vl, PT[pk][:, j * 128: (j + 1) * 128],
                                    j == 0, j == 6))
                    PE(mms, r=["@PTa%d" % pk, "@PTb%d" % pk, "ones_bf", "@Vtok%d" % hl, "@vcsb"],
                       w=["ps%d" % ob, "ps%d" % db])
                    if ii == 3:
                        DVE(lambda v, db=db: v.reciprocal(out=rden, in_=psb[db][:]), r=["ps%d" % db], w=["@rden"])
                        DVE(lambda v, ob=ob, qg=qg, aT=aT: v.tensor_tensor(out=aT[:, qg * 512: (qg + 1) * 512],
                                                                          in0=psb[ob][:], in1=rden, op=ALU.mult),
                            r=["ps%d" % ob, "@rden"], w=[aTr])

                s_stage(0)
                for i in range(16):
                    if i + 1 < 16:
                        s_stage(i + 1)
                    o_stage(i)
                P.dma("sync", cat_d[h], aT, "aT%d" % ((hcount - 1) % 2), r=[aTr], w=["cat%d" % h])
                for _ in range(mod_per_head):
                    for mg in modgens:
                        if next(mg, "done") != "done":
                            break

    def conv_module(l):
        sig = carve(0, [128, 2, T], F32)
        hbufs = [carve(16384, [128, 8, 286], BF16), carve(20992, [128, 8, 286], BF16)]
        dgs = [carve(25600, [128, 31, 128], BF16), carve(33536, [128, 31, 128], BF16)]
        cv = carve(41472, [128, 4, T], F32)
        hsil = carve(74240, [128, 4, T], BF16)
        stg = [carve(90624, [128, 1024], BF16), carve(92672, [128, 1024], BF16)]
        wdw_l = carve(94720, [128, 4, 31], F32)
        tmpa = carve(16384, [128, 512], F32)
        tmpb = carve(18432, [128, 512], F32)
        tmpc = carve(20480, [128, 512], F32)
        sqc = carve(22528, [128, 512], F32)
        P.dma("sync", wdw_l, wdw[:, l], "ld_wdw", w=["@wdw"])
        for pr in range(2):
            wv, wres = load_w(w_in[l][:, 3584 + pr * 256: 3584 + pr * 256 + 256], KC, 256)
            for n in range(2):
                for half in range(2):
                    pair = next_pair()
                    mm_half(wv, wres, n * 128, KC, act_rhs(half), ["act%d" % half], pair)
                    for tt in range(2):
                        t0 = half * 1024 + tt * 512
                        ACT(sig[:, n, t0: t0 + 512], psb[pair[tt]][:], AF.Sigmoid, r=["ps%d" % pair[tt]], w=["@sig%d" % n])
            wv, wres = load_w(w_in[l][:, 3072 + pr * 256: 3072 + pr * 256 + 256], KC, 256)
            for n in range(2):
                hbuf, hr = hbufs[n], "@hbuf%d" % n
                DVE(lambda v, hbuf=hbuf: v.memset(hbuf, 0.0), r=[], w=[hr])
                for half in range(2):
                    pair = next_pair()
                    mm_half(wv, wres, n * 128, KC, act_rhs(half), ["act%d" % half], pair)
                    for tt in range(2):
                        sg0 = half * 4 + tt * 2
                        t0 = half * 1024 + tt * 512
                        DVE(lambda v, n=n, t0=t0, sg0=sg0, pb=psb[pair[tt]], hbuf=hbuf: v.tensor_tensor(
                            out=hbuf[:, sg0: sg0 + 2, 15: 271],
                            in0=pb[:].rearrange("p (a b) -> p a b", a=2),
                            in1=sig[:, n, t0: t0 + 512].rearrange("p (a b) -> p a b", a=2), op=ALU.mult),
                            r=["ps%d" % pair[tt], "@sig%d" % n], w=[hr])
            for n in range(2):
                cc = 2 * pr + n
                hbuf, hr = hbufs[n], "@hbuf%d" % n
                dg, dr = dgs[n], "@dg%d" % n
                DVE(lambda v, hbuf=hbuf: v.tensor_scalar(out=hbuf[:, 1:8, 0:15], in0=hbuf[:, 0:7, 256:271], scalar1=flag_sb[:],
                                                         scalar2=None, op0=ALU.mult), r=[hr, "flag"], w=[hr])
                DVE(lambda v, hbuf=hbuf: v.tensor_scalar(out=hbuf[:, 0:7, 271:286], in0=hbuf[:, 1:8, 15:30], scalar1=flag_sb[:],
                                                         scalar2=None, op0=ALU.mult), r=[hr, "flag"], w=[hr])
                DVE(lambda v, cc=cc, dg=dg: v.tensor_tensor(
                    out=dg, in0=ident[:].unsqueeze(1).to_broadcast([128, 31, 128]),
                    in1=wdw_l[:, cc, :].unsqueeze(2).to_broadcast([128, 31, 128]), op=ALU.mult),
                    r=["ident", "@wdw"], w=[dr])
                for half in range(2):
                    pair = next_pair()
                    mms = []
                    for tt in range(2):
                        sg0 = half * 4 + tt * 2
                        for j in range(31):
                            mms.append((psb[pair[tt]][:].rearrange("p (a b) -> p a b", a=2), dg[:, j, :],
                                        hbuf[:, sg0: sg0 + 2, j: j + 256], j == 0, j == 30))
                    PE(mms, r=[dr, hr], w=["ps%d" % pair[0], "ps%d" % pair[1]])
                    for tt in range(2):
                        t0_ = half * 1024 + tt * 512
                        ACT(cv[:, cc, t0_: t0_ + 512], psb[pair[tt]][:], AF.Identity, r=["ps%d" % pair[tt], "cpar"],
                            w=["@cv%d" % cc], bias=cpar_sb[:, l, 0, cc: cc + 1])
        barrier()
        cvr = ["@cv%d" % c for c in range(4)]
        for tt in range(4):
            sl = slice(tt * 512, (tt + 1) * 512)
            PE([(psb[6][:], ones_f[:], cv[:, cc, sl], cc == 0, cc == 3) for cc in range(4)], r=cvr + ["ones_f"], w=["ps6"])
            for cc in range(4):
                ACT(sqc, cv[:, cc, sl], AF.Square, r=["@cv%d" % cc], w=["@sqc"])
                PE([(psb[7][:], ones_f[:], sqc, cc == 0, cc == 3)], r=["@sqc", "ones_f"], w=["ps7"])
            ACT(tmpa, psb[6][:], AF.Copy, r=["ps6"], w=["@tmpa"], scale=1.0 / 512)
            DVE(lambda v: v.tensor_tensor(out=tmpb, in0=tmpa, in1=tmpa, op=ALU.mult), r=["@tmpa"], w=["@tmpb"])
            DVE(lambda v: v.scalar_tensor_tensor(out=tmpb, in0=psb[7][:], scalar=1.0 / 512, in1=tmpb, op0=ALU.mult,
                                                 op1=ALU.subtract), r=["ps7", "@tmpb"], w=["@tmpb"])
            ACT(tmpb, tmpb, AF.Sqrt, r=["@tmpb", "eps"], w=["@tmpb"], bias=eps_sb[:], scale=1.0)
            DVE(lambda v: v.reciprocal(out=tmpb, in_=tmpb), r=["@tmpb"], w=["@tmpb"])
            for cc in range(4):
                DVE(lambda v, cc=cc, sl=sl: v.tensor_tensor(out=tmpc, in0=cv[:, cc, sl], in1=tmpa, op=ALU.subtract),
                    r=["@cv%d" % cc, "@tmpa"], w=["@tmpc"])
                DVE(lambda v: v.tensor_tensor(out=tmpc, in0=tmpc, in1=tmpb, op=ALU.mult), r=["@tmpc", "@tmpb"], w=["@tmpc"])
                ACT(hsil[:, cc, sl], tmpc, AF.Silu, r=["@tmpc", "cpar"], w=["@hsil"],
                    bias=cpar_sb[:, l, 2, cc: cc + 1], scale=cpar_sb[:, l, 1, cc: cc + 1])
        wv, wres = load_w(w_pw[l], 4, 512)
        n_st = 0
        for co in range(4):
            for half in range(2):
                pair = next_pair()
                mm_half(wv, wres, co * 128, 4, (lambda kc, tt, half=half: hsil[:, kc, half * 1024 + tt * 512: half * 1024 + tt * 512 + 512]),
                        ["@hsil"], pair)
                k = n_st % 2
                n_st += 1
                for tt in range(2):
                    ACT(stg[k][:, tt * 512: (tt + 1) * 512], psb[pair[tt]][:], AF.Identity, r=["ps%d" % pair[tt], "cpar"],
                        w=["@cstg%d" % k], bias=cpar_sb[:, l, 3, co: co + 1])
                P.dma("sync", cat_d[8 + co][:, half * 1024: half * 1024 + 1024], stg[k], "cstg%d" % k,
                      r=["@cstg%d" % k], w=["cat%d" % (8 + co)])

    def pool_mixer(l):
        ub = carve(0, [128, 8, 272], F32)
        p2 = carve(8704, [128, 8, 272], F32)
        p4 = carve(17408, [128, 8, 272], F32)
        p8 = carve(26112, [128, 8, 272], F32)
        p16 = carve(34816, [128, 8, 272], F32)
        icn = carve(43520, [128, 8, 256], F32)
        dbf = carve(51712, [128, T], BF16)
        stg = [carve(55808, [128, 1024], BF16), carve(57856, [128, 1024], BF16)]
        wpool_sb = carve(59904, [128, 4, 128], BF16)
        P.dma("gpsimd", wpool_sb, w_pool[l].rearrange("g i o -> i g o"), "wpool", w=["@wpool"])
        lv = [ub, p2, p4, p8, p16]
        n_st = 0
        for pr in range(2):
            wv, wres = load_w(w_in[l][:, 4096 + pr * 256: 4096 + pr * 256 + 256], KC, 256)
            for n in range(2):
                gi = 2 * pr + n
                DVE(lambda v: v.memset(ub, 0.0), r=[], w=["@ub"])
                P.dma("sync", icn, invcnt[gi].rearrange("p (a b) -> p a b", a=8), "icn", w=["@icn"])
                for half in range(2):
                    pair = next_pair()
                    mm_half(wv, wres, n * 128, KC, act_rhs(half), ["act%d" % half], pair)
                    for tt in range(2):
                        sg0 = half * 4 + tt * 2
                        ACT(ub[:, sg0: sg0 + 2, 8: 264], psb[pair[tt]][:].rearrange("p (a b) -> p a b", a=2), AF.Identity,
                            r=["ps%d" % pair[tt]], w=["@ub"])
                DVE(lambda v: v.tensor_scalar(out=ub[:, 1:8, 0:8], in0=ub[:, 0:7, 256:264], scalar1=flag_sb[:],
                                              scalar2=None, op0=ALU.mult), r=["@ub", "flag"], w=["@ub"])
                DVE(lambda v: v.tensor_scalar(out=ub[:, 0:7, 264:272], in0=ub[:, 1:8, 8:16], scalar1=flag_sb[:],
                                              scalar2=None, op0=ALU.mult), r=["@ub", "flag"], w=["@ub"])
                DVE(lambda v: v.tensor_tensor(out=p2[:, :, 1:272], in0=ub[:, :, 0:271], in1=ub[:, :, 1:272], op=ALU.add),
                    r=["@ub"], w=["@lv"])
                for k in range(2, gi + 2):
                    sh = 1 << (k - 2)
                    lo = (1 << (k - 1))
                    src, dst = lv[k - 1], lv[k]
                    DVE(lambda v, src=src, dst=dst, sh=sh, lo=lo: v.tensor_tensor(
                        out=dst[:, :, lo: 272 - lo], in0=src[:, :, lo - sh: 272 - lo - sh],
                        in1=src[:, :, lo + sh: 272 - lo + sh], op=ALU.add), r=["@lv"], w=["@lv"])
                top = lv[gi + 1]
                DVE(lambda v, top=top: v.tensor_tensor(out=top[:, :, 8:264], in0=top[:, :, 8:264], in1=icn, op=ALU.mult),
                    r=["@lv", "@icn"], w=["@lv"])
                DVE(lambda v, top=top: v.tensor_tensor(out=dbf.rearrange("p (a b) -> p a b", a=8), in0=top[:, :, 8:264],
                                                       in1=ub[:, :, 8:264], op=ALU.subtract),
                    r=["@lv", "@ub"], w=["@dbf"])
                for half in range(2):
                    pair = next_pair()
                    PE([(psb[pair[tt]][:], wpool_sb[:, gi, :], dbf[:, half * 1024 + tt * 512: half * 1024 + tt * 512 + 512],
                         True, True) for tt in range(2)], r=["@wpool", "@dbf"], w=["ps%d" % pair[0], "ps%d" % pair[1]])
                    k = n_st % 2
                    n_st += 1
                    for tt in range(2):
                        ACT(stg[k][:, tt * 512: (tt + 1) * 512], psb[pair[tt]][:], AF.Identity, r=["ps%d" % pair[tt], "cpar"],
                            w=["@pstg%d" % k], scale=cpar_sb[:, l, 4, gi: gi + 1])
                    P.dma("sync", cat_d[12 + gi][:, half * 1024: half * 1024 + 1024], stg[k], "pstg%d" % k,
                          r=["@pstg%d" % k], w=["cat%d" % (12 + gi)])

    def ffn_in(l, half, actb):
        sg = [carve(90112, [128, 1024], F32), carve(94208, [128, 1024], F32)]
        n_sg = 0
        for jb in range(22):
            wg, wgr = load_w(w_ffn_in[l][:, jb * 256: jb * 256 + 256], KC, 256)
            wu, wur = load_w(w_ffn_in[l][:, DFF + jb * 256: DFF + jb * 256 + 256], KC, 256)
            for n in range(2):
                j = jb * 2 + n
                pg = next_pair(3)
                mm_half(wg, wgr, n * 128, KC, act_rhs(half), ["act%d" % half], pg)
                pu = next_pair(3)
                mm_half(wu, wur, n * 128, KC, act_rhs(half), ["act%d" % half], pu)
                k = n_sg % 2
                n_sg += 1
                for tt in range(2):
                    ACT(sg[k][:, tt * 512: (tt + 1) * 512], psb[pg[tt]][:], AF.Silu, r=["ps%d" % pg[tt]], w=["@sg%d" % k])
                    DVE(lambda v, k=k, tt=tt, j=j, pb=psb[pu[tt]]: v.tensor_tensor(
                        out=actb[:, j, tt * 512: (tt + 1) * 512], in0=sg[k][:, tt * 512: (tt + 1) * 512], in1=pb[:],
                        op=ALU.mult), r=["@sg%d" % k, "ps%d" % pu[tt]], w=["@actb"])

    gen0 = modulation(0, split=True)
    for _ in range(16):
        next(gen0)
    for l in range(n_layers):
        xsrc = xT_in if l == 0 else yT
        barrier()
        prenorm(l, 0, [0, 1, 2, 3], xsrc)
        barrier()
        modgens = ([gen0] if l == 0 else []) + ([modulation(l + 1)] if l + 1 < n_layers else [])
        attention(l, modgens, 10 if l == 0 else 6)
        for mg in modgens:
            for _ in mg:
                pass
        barrier()
        conv_module(l)
        barrier()
        pool_mixer(l)
        barrier()
        for half in range(2):
            P.dma("sync", hT[:, :, half * 1024: (half + 1) * 1024],
                  cat_d[:, :, half * 1024: (half + 1) * 1024].rearrange("k p n -> p k n"), "catld%d" % half,
                  r=["cat%d" % c for c in range(KC)], w=["act%d" % half])
        for half in range(2):
            linear_post(l, 0, half, KC, w_out[l], act_rhs(half), ["act%d" % half], xsrc)
        barrier()
        prenorm(l, 1, [0, 1, 2, 3], yT)
        for half in range(2):
            barrier()
            actb = carve(0, [128, KCF, 1024], BF16)
            ffn_in(l, half, actb)
            barrier()
            linear_post(l, 1, half, KCF, w_ffn_out[l],
                        (lambda kc, tt, actb=actb: actb[:, kc, tt * 512: (tt + 1) * 512]), ["@actb"], yT)
    P.emit(nc, es)
    es.close()
    return nc


def _fm(v, nch):
    return np.ascontiguousarray(np.asarray(v, np.float32).reshape(nch, 128).T)


def _bias_tables(rpb_l, sample):
    reps = [0, 1, 2, 3, 14, 15]
    out = np.full((NH, 128, NPAT, 5, 128), NEGM, np.float32)
    kk = np.arange(128)
    qq = np.arange(128)
    for pi, i in enumerate(reps):
        wb0 = wb0_of(i)
        for j in range(5):
            kb = wb0 + j
            if sample:
                kr = (2 * kb + kk // 64)[:, None]
                kcol = (kk % 64)[:, None]
                qr = (2 * i + qq // 64)[None, :]
                qcol = (qq % 64)[None, :]
                sr = np.clip(qr - 4, 0, 24)
                qstart = np.clip(qcol - 8, 0, 48)
                valid = (kr >= sr) & (kr < sr + 8) & (kcol >= qstart) & (kcol < qstart + 16)
                ridx = np.clip(kr - qr + 7, 0, 14)
                cidx = np.clip(kcol - qcol, -15, 15) + 15
                vals = rpb_l[:, ridx, cidx]
                out[:, :, pi, j, :] = np.where(valid[None], vals, np.float32(NEGM))
            else:
                if kb // 2 == i // 2:
                    out[:, :, pi, j, :] = 0.0
    return out.reshape(NH, 128, NPAT, 640)


def _invcnt(sample):
    res = np.zeros((4, T), np.float32)
    L = T if sample else 256
    t = np.arange(T) % L
    for gi, w in enumerate((2, 4, 8, 16)):
        lo = np.clip(t - w // 2, 0, L)
        hi = np.clip(t - w // 2 + w, 0, L)
        res[gi] = 1.0 / (hi - lo).astype(np.float32)
    return np.ascontiguousarray(np.broadcast_to(res[:, None, :], (4, 128, T)))


_NC_CACHE = {}


def kernel(x_prompt, x_sample, cache_k, cache_v, c, c_ctx, w_ada, b_ada, g_pre_mix, g_post_mix,
           g_pre_ffn, g_post_ffn, w_in, rpb, w_dw, b_dw, ln_conv_g, ln_conv_b, w_pw, b_pw,
           w_pool, pool_scale, w_out, w_ffn_in, w_ffn_out):
    f = lambda a: np.ascontiguousarray(np.asarray(a, np.float32))
    x_prompt, x_sample, cache_k, cache_v, c, c_ctx = map(f, (x_prompt, x_sample, cache_k, cache_v, c, c_ctx))
    w_ada, w_in, w_out, w_ffn_in, w_ffn_out, w_pw, w_pool = map(f, (w_ada, w_in, w_out, w_ffn_in, w_ffn_out, w_pw, w_pool))
    rpb = f(rpb)
    b_adaT = np.ascontiguousarray(np.stack([_fm(b_ada[l], 96) for l in range(NL)], axis=1))
    gains = np.ascontiguousarray(np.stack(
        [np.stack([_fm(g[l], KC) for g in (g_pre_mix, g_post_mix, g_pre_ffn, g_post_ffn)], axis=1) for l in range(NL)], axis=1))
    wdw = np.ascontiguousarray(np.stack(
        [np.asarray(w_dw, np.float32)[l, :, 0, :].T.reshape(4, 128, 31).transpose(1, 0, 2) for l in range(NL)], axis=1))
    cpar = np.zeros((128, NL, 6, 4), np.float32)
    for l in range(NL):
        for i, a in enumerate((b_dw, ln_conv_g, ln_conv_b, b_pw, pool_scale)):
            cpar[:, l, i, :] = _fm(np.asarray(a)[l], 4)
    ident = np.eye(128, dtype=np.float32)
    ab_s = np.ascontiguousarray(np.stack([_bias_tables(rpb[l], True) for l in range(NL)]))
    ab_p1 = _bias_tables(rpb[0], False)
    ab_p = np.ascontiguousarray(np.broadcast_to(ab_p1[None], (NL,) + ab_p1.shape))
    ic_s, ic_p = _invcnt(True), _invcnt(False)
    zk = np.zeros((NL, NH, 128, 256), np.float32)
    zv = np.zeros((NL, NH, 128, 2, 128), np.float32)

    in_maps = []
    for core in range(8):
        if core < 4:
            xs = x_sample[core]
            cv = c[core]
            kct = np.ascontiguousarray(cache_k[core].transpose(0, 2, 3, 1))
            vct = np.ascontiguousarray(cache_v[core].reshape(NL, 2, 128, NH, 128).transpose(0, 3, 2, 1, 4))
            abt, ict, cbv, flg = ab_s, ic_s, 0.0, 1.0
        else:
            xs = x_prompt[(core - 4) * 8: (core - 4) * 8 + 8].reshape(T, D)
            cv = c_ctx
            kct, vct = zk, zv
            abt, ict, cbv, flg = ab_p, ic_p, NEGM, 0.0
        in_maps.append({
            "xT_in": np.ascontiguousarray(xs.T).reshape(KC, 128, T),
            "cvec": _fm(cv, KC),
            "w_ada": w_ada, "b_adaT": b_adaT, "gains": gains, "w_in": w_in, "w_out": w_out,
            "w_ffn_in": w_ffn_in, "w_ffn_out": w_ffn_out, "w_pw": w_pw, "w_pool": w_pool,
            "wdw": wdw, "cpar": cpar, "ident": ident, "abias": abt, "kctxT": kct, "vctx": vct,
            "cb": np.full((128, 1), cbv, np.float32), "flag": np.full((128, 1), flg, np.float32),
            "invcnt": ict,
        })
    nl = _NC_CACHE.get("n_layers", NL)
    if ("nc", nl) not in _NC_CACHE:
        _NC_CACHE[("nc", nl)] = build_program(nl)
    res = run_bass_kernel_spmd(_NC_CACHE[("nc", nl)], in_maps, core_ids=list(range(8)))
    outs = res.results
    _NC_CACHE["last_res"] = res
    y_prompt = np.zeros((32, 256, D), np.float32)
    y_sample = np.zeros((4, T, D), np.float32)
    nk = np.zeros((32, NL, 256, NH, 128), np.float32)
    nv = np.zeros((32, NL, 256, NH, 128), np.float32)
    for core in range(8):
        y = np.asarray(outs[core]["yT"]).reshape(D, T).T
        if core < 4:
            y_sample[core] = y
        else:
            b0 = (core - 4) * 8
            y_prompt[b0: b0 + 8] = y.reshape(8, 256, D)
            kk = np.asarray(outs[core]["kT_out"]).reshape(NL, NH, 128, 8, 256).transpose(3, 0, 4, 1, 2)
            vv = np.asarray(outs[core]["vT_out"]).reshape(NL, NH, 128, 8, 256).transpose(3, 0, 4, 1, 2)
            nk[b0: b0 + 8] = kk
            nv[b0: b0 + 8] = vv
    return (y_prompt, y_sample, nk, nv)
```

```python
import contextlib
import numpy as np
import concourse.bass as bass
import concourse.mybir as mybir
from concourse.bass_utils import run_bass_kernel_spmd

F32 = mybir.dt.float32
BF16 = mybir.dt.bfloat16
AF = mybir.ActivationFunctionType
ALU = mybir.AluOpType

D = 2048
T = 2048
KC = 16
NL = 4
DIN = 4608
DFF = 5632
KCF = 44
NH = 8
EPS = 1e-6
NEGM = -30000.0
NPAT = 6
ARENA_BYTES = 100 * 1024
ENGS = ["tensor", "vector", "scalar", "gpsimd", "sync"]


def pat_of(i):
    if i == 0:
        return 0
    if i == 1:
        return 1
    if i == 14:
        return 4
    if i == 15:
        return 5
    return 2 if i % 2 == 0 else 3


def wb0_of(i):
    return min(max(i - 2, 0), 11)


class Op:
    __slots__ = ("eng", "fn", "deps", "idx", "needs_inc", "dma_key", "inc_val")

    def __init__(self, eng, fn, deps, idx, dma_key):
        self.eng = eng
        self.fn = fn
        self.deps = deps
        self.idx = idx
        self.needs_inc = False
        self.dma_key = dma_key
        self.inc_val = 0


class Prog:
    def __init__(self):
        self.ops = {e: [] for e in ENGS}
        self.res = {}
        self.dma_cnt = {}

    def _res(self, name):
        r = self.res.get(name)
        if r is None:
            r = {"w": {}, "r": {}}
            self.res[name] = r
        return r

    @staticmethod
    def _merge(dst, src):
        for k, v in src.items():
            if dst.get(k, -1) < v:
                dst[k] = v

    def op(self, eng, fn, r=(), w=(), dma_key=None):
        r = list(r)
        w = list(w)
        if any(n.startswith("@") for n in r + w):
            r.append("ARENA")
        deps = {}
        for n in r:
            self._merge(deps, self._res(n)["w"])
        for n in w:
            rr = self._res(n)
            self._merge(deps, rr["w"])
            self._merge(deps, rr["r"])
        idx = len(self.ops[eng])
        if eng == "tensor":
            deps.pop(("c", "tensor"), None)
        o = Op(eng, fn, deps, idx, dma_key)
        self.ops[eng].append(o)
        for k, v in deps.items():
            if k[0] == "c":
                self.ops[k[1]][v].needs_inc = True
        if dma_key is not None:
            self.dma_cnt[dma_key] = self.dma_cnt.get(dma_key, 0) + 16
            tok = {("d", dma_key): self.dma_cnt[dma_key]}
        else:
            tok = {("c", eng): idx}
        for n in r:
            self._merge(self._res(n)["r"], tok)
        for n in w:
            rr = self._res(n)
            rr["w"] = dict(tok)
            rr["r"] = {}
        return o

    def dma(self, queue, out, in_, key, r=(), w=()):
        def fn(e, out=out, in_=in_):
            return e.dma_start(out=out, in_=in_)
        return self.op(queue, fn, r, w, dma_key=key)

    def emit(self, nc, es):
        csem = {e: es.enter_context(nc.semaphore("c_" + e)) for e in ENGS}
        dsem = {k: es.enter_context(nc.semaphore("d_%d" % i)) for i, k in enumerate(sorted(self.dma_cnt))}
        for e in ENGS:
            c = 0
            for o in self.ops[e]:
                if o.needs_inc and o.dma_key is None:
                    c += 1
                o.inc_val = c
        block = es.enter_context(nc.Block())
        prog = self

        def run(eng_name, eobj):
            waited = {}
            for o in prog.ops[eng_name]:
                for k, v in o.deps.items():
                    if k[0] == "c":
                        sem = csem[k[1]]
                        val = prog.ops[k[1]][v].inc_val
                    else:
                        sem = dsem[k[1]]
                        val = v
                    if waited.get(k, 0) < val:
                        eobj.wait_ge(sem, val)
                        waited[k] = val
                ins = o.fn(eobj)
                if o.dma_key is not None:
                    ins.then_inc(dsem[o.dma_key], 16)
                elif o.needs_inc:
                    ins.then_inc(csem[eng_name], 1)
            if eng_name == "sync":
                for k, v in prog.dma_cnt.items():
                    eobj.wait_ge(dsem[k], v)

        @block.tensor
        def _(t):
            run("tensor", t)

        @block.vector
        def _(v):
            run("vector", v)

        @block.scalar
        def _(s):
            run("scalar", s)

        @block.gpsimd
        def _(g):
            run("gpsimd", g)

        @block.sync
        def _(sy):
            run("sync", sy)


def build_program(n_layers=NL):
    nc = bass.Bass("TRN2", target_bir_lowering=False)

    def din(name, shape):
        return nc.dram_tensor(name, list(shape), F32, kind="ExternalInput").ap()

    xT_in = din("xT_in", [KC, 128, T])
    cvec = din("cvec", [128, KC])
    w_ada = din("w_ada", [NL, D, 6 * D])
    b_adaT = din("b_adaT", [128, NL, 96])
    gains = din("gains", [128, NL, 4, KC])
    w_in = din("w_in", [NL, D, DIN])
    w_out = din("w_out", [NL, D, D])
    w_ffn_in = din("w_ffn_in", [NL, D, 2 * DFF])
    w_ffn_out = din("w_ffn_out", [NL, DFF, D])
    w_pw = din("w_pw", [NL, 512, 512])
    w_pool = din("w_pool", [NL, 4, 128, 128])
    wdw = din("wdw", [128, NL, 4, 31])
    cpar = din("cpar", [128, NL, 6, 4])
    ident_in = din("ident", [128, 128])
    abias = din("abias", [NL, NH, 128, NPAT, 640])
    kctxT = din("kctxT", [NL, NH, 128, 256])
    vctx = din("vctx", [NL, NH, 128, 2, 128])
    cb_in = din("cb", [128, 1])
    flag_in = din("flag", [128, 1])
    invcnt = din("invcnt", [4, 128, T])

    yT = nc.dram_tensor("yT", [KC, 128, T], F32, kind="ExternalOutput").ap()
    kT_out = nc.dram_tensor("kT_out", [NL, NH, 128, T], F32, kind="ExternalOutput").ap()
    vT_out = nc.dram_tensor("vT_out", [NL, NH, 128, T], F32, kind="ExternalOutput").ap()
    sT_d = nc.dram_tensor("sT_d", [KC, 128, T], F32, kind="Internal").ap()
    cat_d = nc.dram_tensor("cat_d", [KC, 128, T], BF16, kind="Internal").ap()

    P = Prog()
    es = contextlib.ExitStack()

    def sb(name, shape, dt):
        return es.enter_context(nc.sbuf_tensor(name, list(shape), dt))

    hT = sb("hT", [128, KC, T], BF16)
    wsl = [sb("wsl%d" % i, [128, 4096], BF16) for i in range(4)]
    arena = sb("arena", [128, ARENA_BYTES // 2], BF16)
    ones_bf = sb("ones_bf", [128, 128], BF16)
    ones_f = sb("ones_f", [128, 128], F32)
    ident = sb("ident_bf", [128, 128], BF16)
    modT = sb("modT", [128, 96], F32)
    vec = sb("vec", [128, 2, 6, KC], F32)
    gains_sb = sb("gains_sb", [128, NL, 4, KC], F32)
    badaT_sb = sb("badaT_sb", [128, 96], F32)
    cvec_sb = sb("cvec_sb", [128, KC], F32)
    siluc = sb("siluc", [128, KC], F32)
    siluc_rep = sb("siluc_rep", [128, KC, 128], BF16)
    mtmp = sb("mtmp", [128, 2, 128], F32)
    cbt = sb("cbt", [128, 256], BF16)
    cpar_sb = sb("cpar_sb", [128, NL, 6, 4], F32)
    cb_sb = sb("cb_sb", [128, 1], F32)
    flag_sb = sb("flag_sb", [128, 1], F32)
    eps_sb = sb("eps_sb", [128, 1], F32)
    dummy = sb("dummy_t", [128, 8], F32)
    psb = [es.enter_context(nc.psum_tensor("psb%d" % i, [128, 512], F32)) for i in range(8)]

    def carve(off, shape, dt):
        n = int(np.prod(shape[1:]))
        nb = n * (4 if dt == F32 else 2)
        assert off % 4 == 0 and off + nb <= ARENA_BYTES, (off, nb)
        a = arena[:, off // 2: (off + nb) // 2]
        if dt == F32:
            a = a.bitcast(F32)
        if len(shape) == 3:
            a = a.rearrange("p (a b) -> p a b", a=shape[1])
        elif len(shape) == 4:
            a = a.rearrange("p (a b c) -> p a b c", a=shape[1], b=shape[2])
        return a

    def barrier():
        P.op("vector", lambda v: v.memset(dummy[:], 0.0), r=[], w=["ARENA"])

    def PE(mms, r, w):
        def fn(t, mms=mms):
            last = None
            for (o, l, rh, st, sp) in mms:
                last = t.matmul(o, lhsT=l, rhs=rh, start=st, stop=sp)
            return last
        return P.op("tensor", fn, r, w)

    def ACT(out, in_, func, r, w, bias=None, scale=None):
        def fn(s, out=out, in_=in_, func=func, bias=bias, scale=scale):
            kw = {}
            if bias is not None:
                kw["bias"] = bias
            if scale is not None:
                kw["scale"] = scale
            return s.activation(out=out, in_=in_, func=func, **kw)
        return P.op("scalar", fn, r, w)

    def DVE(fn, r, w):
        return P.op("vector", fn, r, w)

    wstate = {"n": 0}

    def load_w(src2d, kcn, ncols, rows0=0):
        s = wstate["n"] % 4
        wstate["n"] += 1
        view = wsl[s][:, 0: kcn * ncols].rearrange("p (k n) -> p k n", k=kcn)
        src = src2d[rows0: rows0 + kcn * 128, :].rearrange("(kc p) n -> p kc n", p=128)
        P.dma("gpsimd", view, src, "w%d" % s, r=[], w=["w%d" % s])
        return view, "w%d" % s

    pair_state = {"n": 0}

    PAIRS3 = [(0, 1), (2, 3), (6, 7)]

    def next_pair(npairs=2):
        pair_state["n"] += 1
        if npairs == 3:
            return PAIRS3[pair_state["n"] % 3]
        p = pair_state["n"] % 2
        return (2 * p, 2 * p + 1)

    def mm_half(wv, wres, n_off, kcn, rhs_of, act_res, pair, first=True, last=True):
        mms = []
        for kc in range(kcn):
            for tt in range(2):
                mms.append((psb[pair[tt]][:], wv[:, kc, n_off: n_off + 128], rhs_of(kc, tt),
                            first and kc == 0, last and kc == kcn - 1))
        PE(mms, r=[wres] + list(act_res), w=["ps%d" % pair[0], "ps%d" % pair[1]])

    def act_rhs(half):
        def f(kc, tt, half=half):
            t0 = half * 1024 + tt * 512
            return hT[:, kc, t0: t0 + 512]
        return f

    P.dma("sync", gains_sb[:], gains, "ld_gains", w=["gains"])
    P.dma("sync", cvec_sb[:], cvec, "ld_cvec", w=["cvec"])
    P.dma("sync", cpar_sb[:], cpar, "ld_cpar", w=["cpar"])
    P.dma("sync", cb_sb[:], cb_in, "ld_cb", w=["cb"])
    P.dma("sync", flag_sb[:], flag_in, "ld_flag", w=["flag"])
    P.dma("gpsimd", ident[:], ident_in, "ld_ident", w=["ident"])
    DVE(lambda v: v.memset(ones_bf[:], 1.0), r=[], w=["ones_bf"])
    DVE(lambda v: v.memset(ones_f[:], 1.0), r=[], w=["ones_f"])
    DVE(lambda v: v.memset(eps_sb[:], EPS), r=[], w=["eps"])
    ACT(siluc[:], cvec_sb[:], AF.Silu, r=["cvec"], w=["siluc"])
    DVE(lambda v: v.memset(cbt[:], 1.0), r=[], w=["cbt"])
    DVE(lambda v: v.tensor_scalar(out=cbt[:], in0=cbt[:], scalar1=cb_sb[:], scalar2=None, op0=ALU.mult),
        r=["cbt", "cb"], w=["cbt"])
    DVE(lambda v: v.tensor_copy(out=siluc_rep[:], in_=siluc[:].unsqueeze(2).to_broadcast([128, KC, 128])),
        r=["siluc"], w=["siluc_rep"])

    def modulation(l):
        for wb in range(48):
            wv, wres = load_w(w_ada[l][:, wb * 256: (wb + 1) * 256], KC, 256)
            PE([(psb[7][:, 0:256], siluc_rep[:, kc, :], wv[:, kc, :], kc == 0, kc == KC - 1) for kc in range(KC)],
               r=[wres, "siluc_rep"], w=["ps7"])
            DVE(lambda v: v.tensor_tensor(out=mtmp[:], in0=psb[7][:, 0:256].rearrange("p (a b) -> p a b", a=2),
                                          in1=ident[:].unsqueeze(1).to_broadcast([128, 2, 128]), op=ALU.mult),
                r=["ps7", "ident"], w=["mtmp"])
            DVE(lambda v, wb=wb: v.tensor_reduce(out=modT[:, 2 * wb: 2 * wb + 2], in_=mtmp[:], axis=mybir.AxisListType.X,
                                                 op=ALU.add), r=["mtmp"], w=["modT"])
            yield
        P.dma("sync", badaT_sb[:], b_adaT[:, l, :], "ld_bada", w=["bada"])
        DVE(lambda v, l=l: v.tensor_tensor(out=modT[:], in0=modT[:], in1=badaT_sb[:], op=ALU.add),
            r=["bada", "modT"], w=["modT"])
        s = l % 2
        vr = "vec%d" % s
        DVE(lambda v, l=l, s=s: v.scalar_tensor_tensor(out=vec[:, s, 0, :], in0=modT[:, 16:32], scalar=1.0,
                                                       in1=gains_sb[:, l, 0, :], op0=ALU.add, op1=ALU.mult),
            r=["modT", "gains"], w=[vr])
        DVE(lambda v, s=s: v.tensor_copy(out=vec[:, s, 1, :], in_=modT[:, 0:16]), r=["modT"], w=[vr])
        DVE(lambda v, l=l, s=s: v.tensor_tensor(out=vec[:, s, 2, :], in0=modT[:, 32:48], in1=gains_sb[:, l, 1, :],
                                                op=ALU.mult), r=["modT", "gains"], w=[vr])
        DVE(lambda v, l=l, s=s: v.scalar_tensor_tensor(out=vec[:, s, 3, :], in0=modT[:, 64:80], scalar=1.0,
                                                       in1=gains_sb[:, l, 2, :], op0=ALU.add, op1=ALU.mult),
            r=["modT", "gains"], w=[vr])
        DVE(lambda v, s=s: v.tensor_copy(out=vec[:, s, 4, :], in_=modT[:, 48:64]), r=["modT"], w=[vr])
        DVE(lambda v, l=l, s=s: v.tensor_tensor(out=vec[:, s, 5, :], in0=modT[:, 80:96], in1=gains_sb[:, l, 3, :],
                                                op=ALU.mult), r=["modT", "gains"], w=[vr])

    def xres(oc, half):
        return "xd%d_%d" % (oc, half)

    def prenorm(l, sub, tts, xsrc):
        s = l % 2
        vr = "vec%d" % s
        ia, ib = (0, 1) if sub == 0 else (3, 4)
        xt = [carve(0, [128, KC, 512], F32), carve(32768, [128, KC, 512], F32)]
        sq = carve(65536, [128, KC, 512], BF16)
        rs = [carve(81920, [128, 512], F32), carve(83968, [128, 512], F32)]
        rstd = [carve(86016, [128, 512], F32), carve(88064, [128, 512], F32)]

        def stage_a(n):
            tt = tts[n]
            k = n % 2
            half = tt // 2
            xr = "@xt%d" % k
            P.dma("sync", xt[k], xsrc[:, :, tt * 512: (tt + 1) * 512].rearrange("k p n -> p k n"), "xt%d" % k,
                  r=[xres(oc, half) for oc in range(KC)], w=[xr])
            ACT(sq[:, 0:8, :], xt[k][:, 0:8, :], AF.Square, r=[xr], w=["@sqa"])
            P.op("gpsimd", lambda g, k=k: g.tensor_tensor(out=sq[:, 8:16, :], in0=xt[k][:, 8:16, :], in1=xt[k][:, 8:16, :],
                                                         op=ALU.mult), r=[xr], w=["@sqb"])
            bank = 6 + k
            PE([(psb[bank][:], ones_bf[:], sq[:, kc, :], kc == 0, kc == KC - 1) for kc in range(KC)],
               r=["@sqa", "@sqb", "ones_bf"], w=["ps%d" % bank])
            ACT(rs[k], psb[bank][:], AF.Sqrt, r=["ps%d" % bank, "eps"], w=["@rs%d" % k], bias=eps_sb[:], scale=1.0 / D)

        def stage_b(n):
            k = n % 2
            xr = "@xt%d" % k
            DVE(lambda v, k=k: v.reciprocal(out=rstd[k], in_=rs[k]), r=["@rs%d" % k], w=["@rstd%d" % k])
            DVE(lambda v, k=k: v.tensor_tensor(out=xt[k], in0=xt[k],
                                               in1=rstd[k].unsqueeze(1).to_broadcast([128, KC, 512]), op=ALU.mult),
                r=[xr, "@rstd%d" % k], w=[xr])

        def stage_c(n):
            tt = tts[n]
            k = n % 2
            half = tt // 2
            xr = "@xt%d" % k
            for kc in range(KC):
                ACT(hT[:, kc, tt * 512: (tt + 1) * 512], xt[k][:, kc, :], AF.Identity, r=[xr, vr],
                    w=["act%d" % half], bias=vec[:, s, ib, kc: kc + 1], scale=vec[:, s, ia, kc: kc + 1])

        nt = len(tts)
        stage_a(0)
        for n in range(nt):
            if n + 1 < nt:
                stage_a(n + 1)
            stage_b(n)
            stage_c(n)

    def linear_post(l, sub, half, kcn_total, wsrc, rhs_of, act_res, xsrc):
        s = l % 2
        vr = "vec%d" % s
        ig = 2 if sub == 0 else 5
        base = 90112
        stg = [carve(base, [128, 1024], F32), carve(base + 4096, [128, 1024], F32)]
        sqo = carve(base + 8192, [128, 1024], BF16)
        b6, b7 = 4, 5
        t0 = half * 1024
        keep = (sub == 0)
        obuf = carve(0, [128, KC, 1024], F32) if keep else None
        cur = None
        for oc in range(KC):
            pair = next_pair(3)
            c0 = (oc // 2) * 256
            if kcn_total == KC:
                if oc % 2 == 0:
                    cur = [load_w(wsrc[:, c0: c0 + 256], KC, 256) + (0, KC)]
            else:
                if oc % 2 == 0:
                    cur = [load_w(wsrc[:, c0: c0 + 256], kn, 256, rows0=r0) + (r0 // 128, kn)
                           for (r0, kn) in ((0, 16), (2048, 16), (4096, 12))]
            for g, (wv, wres, kc0, kn) in enumerate(cur):
                mm_half(wv, wres, (oc % 2) * 128, kn,
                        (lambda kc, tt, kc0=kc0: rhs_of(kc0 + kc, tt)), act_res, pair,
                        first=(g == 0), last=(g == len(cur) - 1))
            k = oc % 2
            if keep:
                dst, sr = obuf[:, oc, :], "@ob%d" % oc
            else:
                dst, sr = stg[k], "@stg%d" % k
            for tt in range(2):
                ACT(dst[:, tt * 512: (tt + 1) * 512], psb[pair[tt]][:], AF.Identity,
                    r=["ps%d" % pair[tt]], w=[sr])
            if not keep:
                P.dma("sync", sT_d[oc][:, t0: t0 + 1024], stg[k], "stg%d" % k, r=[sr], w=["sd%d_%d" % (oc, half)])
            ACT(sqo, dst, AF.Square, r=[sr], w=["@sqo"])
            PE([(psb[b6 + tt][:], ones_bf[:], sqo[:, tt * 512: (tt + 1) * 512], oc == 0, oc == KC - 1)
                for tt in range(2)], r=["@sqo", "ones_bf"], w=["ps%d" % b6, "ps%d" % b7])
        if keep:
            rso, rsr = carve(81920 + half * 4096, [128, 1024], F32), "@rsp%d" % half
        else:
            rso, rsr = carve(base + 8192, [128, 1024], F32), "@sqo"
            barrier()

        def rstd_chain():
            for tt in range(2):
                ACT(rso[:, tt * 512: (tt + 1) * 512], psb[b6 + tt][:], AF.Sqrt, r=["ps%d" % (b6 + tt), rsr],
                    w=[rsr], bias=eps_sb[:], scale=1.0 / D)
            DVE(lambda v: v.reciprocal(out=rso, in_=rso), r=[rsr], w=[rsr])
        if keep:
            xb = [carve(65536, [128, 2, 1024], F32), carve(73728, [128, 2, 1024], F32), carve(90112, [128, 2, 1024], F32)]
            fb = None
        else:
            xb = [carve(32768 + i * 8192, [128, 2, 1024], F32) for i in range(4)]
            fb = [carve(i * 8192, [128, 2, 1024], F32) for i in range(4)]
        nb = len(xb)

        def loads(g):
            k = g % nb
            P.dma("sync", xb[k], xsrc[2 * g: 2 * g + 2, :, t0: t0 + 1024].rearrange("k p n -> p k n"), "xb%d" % k,
                  r=[xres(2 * g, half), xres(2 * g + 1, half)], w=["@xb%d" % k])
            if not keep:
                P.dma("sync", fb[k], sT_d[2 * g: 2 * g + 2, :, t0: t0 + 1024].rearrange("k p n -> p k n"), "fb%d" % k,
                      r=["sd%d_%d" % (2 * g, half), "sd%d_%d" % (2 * g + 1, half)], w=["@fb%d_0" % k, "@fb%d_1" % k])

        for g in range(nb - 1):
            loads(g)
        rstd_chain()
        for g in range(8):
            k = g % nb
            if g + nb - 1 < 8:
                loads(g + nb - 1)
            if keep:
                src = obuf[:, 2 * g: 2 * g + 2, :]
                srs = ["@ob%d" % (2 * g), "@ob%d" % (2 * g + 1)]
            else:
                src = fb[k]
                srs = ["@fb%d_0" % k, "@fb%d_1" % k]
            for n in range(2):
                oc = 2 * g + n
                DVE(lambda v, src=src, n=n, oc=oc: v.scalar_tensor_tensor(
                    out=src[:, n, :], in0=src[:, n, :], scalar=vec[:, s, ig, oc: oc + 1], in1=rso,
                    op0=ALU.mult, op1=ALU.mult), r=[srs[n], rsr, vr], w=[srs[n]])
            if g % 2 == 0:
                P.op("gpsimd", lambda e, k=k, src=src: e.tensor_tensor(out=xb[k], in0=xb[k], in1=src, op=ALU.add),
                     r=srs + ["@xb%d" % k], w=["@xb%d" % k])
            else:
                DVE(lambda v, k=k, src=src: v.tensor_tensor(out=xb[k], in0=xb[k], in1=src, op=ALU.add),
                    r=srs + ["@xb%d" % k], w=["@xb%d" % k])
            P.dma("gpsimd" if keep else "scalar", yT[2 * g: 2 * g + 2, :, t0: t0 + 1024].rearrange("k p n -> p k n"), xb[k], "xbs%d" % k,
                  r=["@xb%d" % k], w=[xres(2 * g, half), xres(2 * g + 1, half)])

    def attention(l, modgen):
        KT = carve(0, [128, 4, T], BF16)
        QT = carve(16384, [128, 4, T], BF16)
        Vtok = carve(32768, [128, 4, 16, 128], BF16)
        vTb = carve(49152, [128, T], BF16)
        aTs = [carve(53248, [128, T], BF16), carve(57344, [128, T], BF16)]
        stg = [carve(61440, [128, 1024], F32), carve(65536, [128, 1024], F32)]
        ab = [carve(69632, [128, NPAT, 640], BF16), carve(77312, [128, NPAT, 640], BF16)]
        kc_sb = carve(84992, [128, 4, 256], BF16)
        vc_sb = carve(87040, [128, 4, 2, 128], BF16)
        PT = [carve(89088, [128, 896], BF16), carve(90880, [128, 896], BF16)]
        rden = carve(92672, [128, 512], F32)
        scale = 1.0 / np.sqrt(128.0)
        stn = {"n": 0}
        abn = {"n": 0}
        ptn = {"n": 0}
        hcount = 0
        for g in range(2):
            P.dma("gpsimd", kc_sb, kctxT[l, 4 * g: 4 * g + 4].rearrange("h p n -> p h n"), "kcsb", w=["@kcsb"])
            P.dma("gpsimd", vc_sb, vctx[l, 4 * g: 4 * g + 4].rearrange("h p b d -> p h b d"), "vcsb", w=["@vcsb"])
            for which, c0 in (("k", 1024 + 512 * g), ("v", 2048 + 512 * g), ("q", 512 * g)):
                for blk in range(2):
                    wv, wres = load_w(w_in[l][:, c0 + blk * 256: c0 + blk * 256 + 256], KC, 256)
                    for n in range(2):
                        hl = blk * 2 + n
                        h = 4 * g + hl
                        for half in range(2):
                            pair = next_pair()
                            mm_half(wv, wres, n * 128, KC, act_rhs(half), ["act%d" % half], pair)
                            t0 = half * 1024
                            if which == "q":
                                for tt in range(2):
                                    ACT(QT[:, hl, t0 + tt * 512: t0 + tt * 512 + 512], psb[pair[tt]][:], AF.Copy,
                                        r=["ps%d" % pair[tt]], w=["@QT%d" % hl], scale=float(scale))
                            else:
                                k = stn["n"] % 2
                                stn["n"] += 1
                                sr = "@astg%d" % k
                                for tt in range(2):
                                    ACT(stg[k][:, tt * 512: (tt + 1) * 512], psb[pair[tt]][:], AF.Identity,
                                        r=["ps%d" % pair[tt]], w=[sr])
                                dst = (kT_out if which == "k" else vT_out)[l, h][:, t0: t0 + 1024]
                                P.dma("sync", dst, stg[k], "astg%d" % k, r=[sr], w=["kvout_%s_%d_%d" % (which, h, half)])
                                if which == "k":
                                    DVE(lambda v, k=k, hl=hl, t0=t0: v.tensor_copy(out=KT[:, hl, t0: t0 + 1024], in_=stg[k]),
                                        r=[sr], w=["@KT%d" % hl])
                                else:
                                    DVE(lambda v, k=k, t0=t0: v.tensor_copy(out=vTb[:, t0: t0 + 1024], in_=stg[k]),
                                        r=[sr], w=["@vTb"])
                                    pb = psb[6 + half]
                                    pbv = pb[:].bitcast(BF16)
                                    def tfn(t, t0=t0, pbv=pbv):
                                        last = None
                                        for b in range(8):
                                            last = t.transpose(pbv[:, b * 128: (b + 1) * 128],
                                                               vTb[:, t0 + b * 128: t0 + (b + 1) * 128], ident[:])
                                        return last
                                    P.op("tensor", tfn, r=["@vTb", "ident"], w=["ps%d" % (6 + half)])
                                    DVE(lambda v, hl=hl, half=half, pbv=pbv: v.tensor_copy(
                                        out=Vtok[:, hl, half * 8: half * 8 + 8, :].rearrange("p a b -> p (a b)"), in_=pbv),
                                        r=["ps%d" % (6 + half)], w=["@Vtok%d" % hl])
            for hl in range(4):
                h = 4 * g + hl
                ak = abn["n"] % 2
                abn["n"] += 1
                abr = "@ab%d" % ak
                P.dma("gpsimd", ab[ak], abias[l, h], "ab%d" % ak, w=[abr])
                aT = aTs[hcount % 2]
                aTr = "@aT%d" % (hcount % 2)
                hcount += 1
                def s_stage(i, hl=hl, ak=ak, abr=abr):
                    pid = pat_of(i)
                    wb0 = wb0_of(i)
                    pair = (0, 1) if i % 2 == 0 else (2, 3)
                    bA, bB = psb[pair[0]], psb[pair[1]]
                    q_ap = QT[:, hl, i * 128: (i + 1) * 128]
                    mms = [(bA[:, 0:512], ident[:], ab[ak][:, pid, 0:512], True, False)]
                    for j in range(4):
                        kb = wb0 + j
                        mms.append((bA[:, j * 128: (j + 1) * 128], KT[:, hl, kb * 128: (kb + 1) * 128], q_ap,
                                    False, j == 3))
                    mms.append((bB[:, 0:128], ident[:], ab[ak][:, pid, 512:640], True, False))
                    mms.append((bB[:, 128:384], ident[:], cbt[:], False, False))
                    kb = wb0 + 4
                    mms.append((bB[:, 0:128], KT[:, hl, kb * 128: (kb + 1) * 128], q_ap, False, False))
                    for cbk in range(2):
                        mms.append((bB[:, 128 + cbk * 128: 256 + cbk * 128], kc_sb[:, hl, cbk * 128: (cbk + 1) * 128],
                                    q_ap, False, cbk == 1))
                    PE(mms, r=[abr, "ident", "cbt", "@KT%d" % hl, "@QT%d" % hl, "@kcsb"],
                       w=["ps%d" % pair[0], "ps%d" % pair[1]])
                    pk = i % 2
                    ACT(PT[pk][:, 0:512], bA[:], AF.Exp, r=["ps%d" % pair[0]], w=["@PTa%d" % pk])
                    ACT(PT[pk][:, 512:896], bB[:, 0:384], AF.Exp, r=["ps%d" % pair[1]], w=["@PTb%d" % pk])

                def o_stage(i, hl=hl, aT=aT, aTr=aTr):
                    qg, ii = i // 4, i % 4
                    wb0 = wb0_of(i)
                    ob, db = (4, 5) if qg % 2 == 0 else (6, 7)
                    pk = i % 2
                    mms = []
                    for j in range(7):
                        mms.append((psb[db][:, ii * 128: (ii + 1) * 128], ones_bf[:], PT[pk][:, j * 128: (j + 1) * 128],
                                    j == 0, j == 6))
                    for j in range(7):
                        if j < 5:
                            vl = Vtok[:, hl, wb0 + j, :]
                        else:
                            vl = vc_sb[:, hl, j - 5, :]
                        mms.append((psb[ob][:, ii * 128: (ii + 1) * 128], vl, PT[pk][:, j * 128: (j + 1) * 128],
                                    j == 0, j == 6))
                    PE(mms, r=["@PTa%d" % pk, "@PTb%d" % pk, "ones_bf", "@Vtok%d" % hl, "@vcsb"],
                       w=["ps%d" % ob, "ps%d" % db])
                    if ii == 3:
                        DVE(lambda v, db=db: v.reciprocal(out=rden, in_=psb[db][:]), r=["ps%d" % db], w=["@rden"])
                        DVE(lambda v, ob=ob, qg=qg, aT=aT: v.tensor_tensor(out=aT[:, qg * 512: (qg + 1) * 512],
                                                                          in0=psb[ob][:], in1=rden, op=ALU.mult),
                            r=["ps%d" % ob, "@rden"], w=[aTr])

                s_stage(0)
                for i in range(16):
                    if i + 1 < 16:
                        s_stage(i + 1)
                    o_stage(i)
                P.dma("sync", cat_d[h], aT, "aT%d" % ((hcount - 1) % 2), r=[aTr], w=["cat%d" % h])
                if modgen is not None:
                    for _ in range(6):
                        next(modgen, None)

    def conv_module(l):
        sig = carve(0, [128, 2, T], F32)
        hbufs = [carve(16384, [128, 8, 286], BF16), carve(20992, [128, 8, 286], BF16)]
        dgs = [carve(25600, [128, 31, 128], BF16), carve(33536, [128, 31, 128], BF16)]
        cv = carve(41472, [128, 4, T], F32)
        hsil = carve(74240, [128, 4, T], BF16)
        stg = [carve(90624, [128, 1024], BF16), carve(92672, [128, 1024], BF16)]
        wdw_l = carve(94720, [128, 4, 31], F32)
        tmpa = carve(16384, [128, 512], F32)
        tmpb = carve(18432, [128, 512], F32)
        tmpc = carve(20480, [128, 512], F32)
        sqc = carve(22528, [128, 512], F32)
        P.dma("sync", wdw_l, wdw[:, l], "ld_wdw", w=["@wdw"])
        for pr in range(2):
            wv, wres = load_w(w_in[l][:, 3584 + pr * 256: 3584 + pr * 256 + 256], KC, 256)
            for n in range(2):
                for half in range(2):
                    pair = next_pair()
                    mm_half(wv, wres, n * 128, KC, act_rhs(half), ["act%d" % half], pair)
                    for tt in range(2):
                        t0 = half * 1024 + tt * 512
                        ACT(sig[:, n, t0: t0 + 512], psb[pair[tt]][:], AF.Sigmoid, r=["ps%d" % pair[tt]], w=["@sig%d" % n])
            wv, wres = load_w(w_in[l][:, 3072 + pr * 256: 3072 + pr * 256 + 256], KC, 256)
            for n in range(2):
                hbuf, hr = hbufs[n], "@hbuf%d" % n
                DVE(lambda v, hbuf=hbuf: v.memset(hbuf, 0.0), r=[], w=[hr])
                for half in range(2):
                    pair = next_pair()
                    mm_half(wv, wres, n * 128, KC, act_rhs(half), ["act%d" % half], pair)
                    for tt in range(2):
                        sg0 = half * 4 + tt * 2
                        t0 = half * 1024 + tt * 512
                        DVE(lambda v, n=n, t0=t0, sg0=sg0, pb=psb[pair[tt]], hbuf=hbuf: v.tensor_tensor(
                            out=hbuf[:, sg0: sg0 + 2, 15: 271],
                            in0=pb[:].rearrange("p (a b) -> p a b", a=2),
                            in1=sig[:, n, t0: t0 + 512].rearrange("p (a b) -> p a b", a=2), op=ALU.mult),
                            r=["ps%d" % pair[tt], "@sig%d" % n], w=[hr])
            for n in range(2):
                cc = 2 * pr + n
                hbuf, hr = hbufs[n], "@hbuf%d" % n
                dg, dr = dgs[n], "@dg%d" % n
                DVE(lambda v, hbuf=hbuf: v.tensor_scalar(out=hbuf[:, 1:8, 0:15], in0=hbuf[:, 0:7, 256:271], scalar1=flag_sb[:],
                                                         scalar2=None, op0=ALU.mult), r=[hr, "flag"], w=[hr])
                DVE(lambda v, hbuf=hbuf: v.tensor_scalar(out=hbuf[:, 0:7, 271:286], in0=hbuf[:, 1:8, 15:30], scalar1=flag_sb[:],
                                                         scalar2=None, op0=ALU.mult), r=[hr, "flag"], w=[hr])
                DVE(lambda v, cc=cc, dg=dg: v.tensor_tensor(
                    out=dg, in0=ident[:].unsqueeze(1).to_broadcast([128, 31, 128]),
                    in1=wdw_l[:, cc, :].unsqueeze(2).to_broadcast([128, 31, 128]), op=ALU.mult),
                    r=["ident", "@wdw"], w=[dr])
                for half in range(2):
                    pair = next_pair()
                    mms = []
                    for tt in range(2):
                        sg0 = half * 4 + tt * 2
                        for j in range(31):
                            mms.append((psb[pair[tt]][:].rearrange("p (a b) -> p a b", a=2), dg[:, j, :],
                                        hbuf[:, sg0: sg0 + 2, j: j + 256], j == 0, j == 30))
                    PE(mms, r=[dr, hr], w=["ps%d" % pair[0], "ps%d" % pair[1]])
                    for tt in range(2):
                        t0_ = half * 1024 + tt * 512
                        ACT(cv[:, cc, t0_: t0_ + 512], psb[pair[tt]][:], AF.Identity, r=["ps%d" % pair[tt], "cpar"],
                            w=["@cv%d" % cc], bias=cpar_sb[:, l, 0, cc: cc + 1])
        barrier()
        cvr = ["@cv%d" % c for c in range(4)]
        for tt in range(4):
            sl = slice(tt * 512, (tt + 1) * 512)
            PE([(psb[6][:], ones_f[:], cv[:, cc, sl], cc == 0, cc == 3) for cc in range(4)], r=cvr + ["ones_f"], w=["ps6"])
            for cc in range(4):
                ACT(sqc, cv[:, cc, sl], AF.Square, r=["@cv%d" % cc], w=["@sqc"])
                PE([(psb[7][:], ones_f[:], sqc, cc == 0, cc == 3)], r=["@sqc", "ones_f"], w=["ps7"])
            ACT(tmpa, psb[6][:], AF.Copy, r=["ps6"], w=["@tmpa"], scale=1.0 / 512)
            DVE(lambda v: v.tensor_tensor(out=tmpb, in0=tmpa, in1=tmpa, op=ALU.mult), r=["@tmpa"], w=["@tmpb"])
            DVE(lambda v: v.scalar_tensor_tensor(out=tmpb, in0=psb[7][:], scalar=1.0 / 512, in1=tmpb, op0=ALU.mult,
                                                 op1=ALU.subtract), r=["ps7", "@tmpb"], w=["@tmpb"])
            ACT(tmpb, tmpb, AF.Sqrt, r=["@tmpb", "eps"], w=["@tmpb"], bias=eps_sb[:], scale=1.0)
            DVE(lambda v: v.reciprocal(out=tmpb, in_=tmpb), r=["@tmpb"], w=["@tmpb"])
            for cc in range(4):
                DVE(lambda v, cc=cc, sl=sl: v.tensor_tensor(out=tmpc, in0=cv[:, cc, sl], in1=tmpa, op=ALU.subtract),
                    r=["@cv%d" % cc, "@tmpa"], w=["@tmpc"])
                DVE(lambda v: v.tensor_tensor(out=tmpc, in0=tmpc, in1=tmpb, op=ALU.mult), r=["@tmpc", "@tmpb"], w=["@tmpc"])
                ACT(hsil[:, cc, sl], tmpc, AF.Silu, r=["@tmpc", "cpar"], w=["@hsil"],
                    bias=cpar_sb[:, l, 2, cc: cc + 1], scale=cpar_sb[:, l, 1, cc: cc + 1])
        wv, wres = load_w(w_pw[l], 4, 512)
        n_st = 0
        for co in range(4):
            for half in range(2):
                pair = next_pair()
                mm_half(wv, wres, co * 128, 4, (lambda kc, tt, half=half: hsil[:, kc, half * 1024 + tt * 512: half * 1024 + tt * 512 + 512]),
                        ["@hsil"], pair)
                k = n_st % 2
                n_st += 1
                for tt in range(2):
                    ACT(stg[k][:, tt * 512: (tt + 1) * 512], psb[pair[tt]][:], AF.Identity, r=["ps%d" % pair[tt], "cpar"],
                        w=["@cstg%d" % k], bias=cpar_sb[:, l, 3, co: co + 1])
                P.dma("sync", cat_d[8 + co][:, half * 1024: half * 1024 + 1024], stg[k], "cstg%d" % k,
                      r=["@cstg%d" % k], w=["cat%d" % (8 + co)])

    def pool_mixer(l):
        ub = carve(0, [128, 8, 272], F32)
        p2 = carve(8704, [128, 8, 272], F32)
        p4 = carve(17408, [128, 8, 272], F32)
        p8 = carve(26112, [128, 8, 272], F32)
        p16 = carve(34816, [128, 8, 272], F32)
        icn = carve(43520, [128, 8, 256], F32)
        dbf = carve(51712, [128, T], BF16)
        stg = [carve(55808, [128, 1024], BF16), carve(57856, [128, 1024], BF16)]
        wpool_sb = carve(59904, [128, 4, 128], BF16)
        P.dma("gpsimd", wpool_sb, w_pool[l].rearrange("g i o -> i g o"), "wpool", w=["@wpool"])
        lv = [ub, p2, p4, p8, p16]
        n_st = 0
        for pr in range(2):
            wv, wres = load_w(w_in[l][:, 4096 + pr * 256: 4096 + pr * 256 + 256], KC, 256)
            for n in range(2):
                gi = 2 * pr + n
                DVE(lambda v: v.memset(ub, 0.0), r=[], w=["@ub"])
                P.dma("sync", icn, invcnt[gi].rearrange("p (a b) -> p a b", a=8), "icn", w=["@icn"])
                for half in range(2):
                    pair = next_pair()
                    mm_half(wv, wres, n * 128, KC, act_rhs(half), ["act%d" % half], pair)
                    for tt in range(2):
                        sg0 = half * 4 + tt * 2
                        ACT(ub[:, sg0: sg0 + 2, 8: 264], psb[pair[tt]][:].rearrange("p (a b) -> p a b", a=2), AF.Identity,
                            r=["ps%d" % pair[tt]], w=["@ub"])
                DVE(lambda v: v.tensor_scalar(out=ub[:, 1:8, 0:8], in0=ub[:, 0:7, 256:264], scalar1=flag_sb[:],
                                              scalar2=None, op0=ALU.mult), r=["@ub", "flag"], w=["@ub"])
                DVE(lambda v: v.tensor_scalar(out=ub[:, 0:7, 264:272], in0=ub[:, 1:8, 8:16], scalar1=flag_sb[:],
                                              scalar2=None, op0=ALU.mult), r=["@ub", "flag"], w=["@ub"])
                DVE(lambda v: v.tensor_tensor(out=p2[:, :, 1:272], in0=ub[:, :, 0:271], in1=ub[:, :, 1:272], op=ALU.add),
                    r=["@ub"], w=["@lv"])
                for k in range(2, gi + 2):
                    sh = 1 << (k - 2)
                    lo = (1 << (k - 1))
                    src, dst = lv[k - 1], lv[k]
                    DVE(lambda v, src=src, dst=dst, sh=sh, lo=lo: v.tensor_tensor(
                        out=dst[:, :, lo: 272 - lo], in0=src[:, :, lo - sh: 272 - lo - sh],
                        in1=src[:, :, lo + sh: 272 - lo + sh], op=ALU.add), r=["@lv"], w=["@lv"])
                top = lv[gi + 1]
                DVE(lambda v, top=top: v.tensor_tensor(out=top[:, :, 8:264], in0=top[:, :, 8:264], in1=icn, op=ALU.mult),
                    r=["@lv", "@icn"], w=["@lv"])
                DVE(lambda v, top=top: v.tensor_tensor(out=dbf.rearrange("p (a b) -> p a b", a=8), in0=top[:, :, 8:264],
                                                       in1=ub[:, :, 8:264], op=ALU.subtract),
                    r=["@lv", "@ub"], w=["@dbf"])
                for half in range(2):
                    pair = next_pair()
                    PE([(psb[pair[tt]][:], wpool_sb[:, gi, :], dbf[:, half * 1024 + tt * 512: half * 1024 + tt * 512 + 512],
                         True, True) for tt in range(2)], r=["@wpool", "@dbf"], w=["ps%d" % pair[0], "ps%d" % pair[1]])
                    k = n_st % 2
                    n_st += 1
                    for tt in range(2):
                        ACT(stg[k][:, tt * 512: (tt + 1) * 512], psb[pair[tt]][:], AF.Identity, r=["ps%d" % pair[tt], "cpar"],
                            w=["@pstg%d" % k], scale=cpar_sb[:, l, 4, gi: gi + 1])
                    P.dma("sync", cat_d[12 + gi][:, half * 1024: half * 1024 + 1024], stg[k], "pstg%d" % k,
                          r=["@pstg%d" % k], w=["cat%d" % (12 + gi)])

    def ffn_in(l, half, actb):
        sg = [carve(90112, [128, 1024], F32), carve(94208, [128, 1024], F32)]
        n_sg = 0
        for jb in range(22):
            wg, wgr = load_w(w_ffn_in[l][:, jb * 256: jb * 256 + 256], KC, 256)
            wu, wur = load_w(w_ffn_in[l][:, DFF + jb * 256: DFF + jb * 256 + 256], KC, 256)
            for n in range(2):
                j = jb * 2 + n
                pg = next_pair(3)
                mm_half(wg, wgr, n * 128, KC, act_rhs(half), ["act%d" % half], pg)
                pu = next_pair(3)
                mm_half(wu, wur, n * 128, KC, act_rhs(half), ["act%d" % half], pu)
                k = n_sg % 2
                n_sg += 1
                for tt in range(2):
                    ACT(sg[k][:, tt * 512: (tt + 1) * 512], psb[pg[tt]][:], AF.Silu, r=["ps%d" % pg[tt]], w=["@sg%d" % k])
                    DVE(lambda v, k=k, tt=tt, j=j, pb=psb[pu[tt]]: v.tensor_tensor(
                        out=actb[:, j, tt * 512: (tt + 1) * 512], in0=sg[k][:, tt * 512: (tt + 1) * 512], in1=pb[:],
                        op=ALU.mult), r=["@sg%d" % k, "ps%d" % pu[tt]], w=["@actb"])

    for _ in modulation(0):
        pass
    for l in range(n_layers):
        xsrc = xT_in if l == 0 else yT
        barrier()
        prenorm(l, 0, [0, 1, 2, 3], xsrc)
        barrier()
        modgen = modulation(l + 1) if l + 1 < n_layers else None
        attention(l, modgen)
        if modgen is not None:
            for _ in modgen:
                pass
        barrier()
        conv_module(l)
        barrier()
        pool_mixer(l)
        barrier()
        for half in range(2):
            P.dma("sync", hT[:, :, half * 1024: (half + 1) * 1024],
                  cat_d[:, :, half * 1024: (half + 1) * 1024].rearrange("k p n -> p k n"), "catld%d" % half,
                  r=["cat%d" % c for c in range(KC)], w=["act%d" % half])
        for half in range(2):
            linear_post(l, 0, half, KC, w_out[l], act_rhs(half), ["act%d" % half], xsrc)
        barrier()
        prenorm(l, 1, [0, 1, 2, 3], yT)
        for half in range(2):
            barrier()
            actb = carve(0, [128, KCF, 1024], BF16)
            ffn_in(l, half, actb)
            barrier()
            linear_post(l, 1, half, KCF, w_ffn_out[l],
                        (lambda kc, tt, actb=actb: actb[:, kc, tt * 512: (tt + 1) * 512]), ["@actb"], yT)
    P.emit(nc, es)
    es.close()
    return nc


def _fm(v, nch):
    return np.ascontiguousarray(np.asarray(v, np.float32).reshape(nch, 128).T)


def _bias_tables(rpb_l, sample):
    reps = [0, 1, 2, 3, 14, 15]
    out = np.full((NH, 128, NPAT, 5, 128), NEGM, np.float32)
    kk = np.arange(128)
    qq = np.arange(128)
    for pi, i in enumerate(reps):
        wb0 = wb0_of(i)
        for j in range(5):
            kb = wb0 + j
            if sample:
                kr = (2 * kb + kk // 64)[:, None]
                kcol = (kk % 64)[:, None]
                qr = (2 * i + qq // 64)[None, :]
                qcol = (qq % 64)[None, :]
                sr = np.clip(qr - 4, 0, 24)
                qstart = np.clip(qcol - 8, 0, 48)
                valid = (kr >= sr) & (kr < sr + 8) & (kcol >= qstart) & (kcol < qstart + 16)
                ridx = np.clip(kr - qr + 7, 0, 14)
                cidx = np.clip(kcol - qcol, -15, 15) + 15
                vals = rpb_l[:, ridx, cidx]
                out[:, :, pi, j, :] = np.where(valid[None], vals, np.float32(NEGM))
            else:
                if kb // 2 == i // 2:
                    out[:, :, pi, j, :] = 0.0
    return out.reshape(NH, 128, NPAT, 640)


def _invcnt(sample):
    res = np.zeros((4, T), np.float32)
    L = T if sample else 256
    t = np.arange(T) % L
    for gi, w in enumerate((2, 4, 8, 16)):
        lo = np.clip(t - w // 2, 0, L)
        hi = np.clip(t - w // 2 + w, 0, L)
        res[gi] = 1.0 / (hi - lo).astype(np.float32)
    return np.ascontiguousarray(np.broadcast_to(res[:, None, :], (4, 128, T)))


_NC_CACHE = {}


def kernel(x_prompt, x_sample, cache_k, cache_v, c, c_ctx, w_ada, b_ada, g_pre_mix, g_post_mix,
           g_pre_ffn, g_post_ffn, w_in, rpb, w_dw, b_dw, ln_conv_g, ln_conv_b, w_pw, b_pw,
           w_pool, pool_scale, w_out, w_ffn_in, w_ffn_out):
    f = lambda a: np.ascontiguousarray(np.asarray(a, np.float32))
    x_prompt, x_sample, cache_k, cache_v, c, c_ctx = map(f, (x_prompt, x_sample, cache_k, cache_v, c, c_ctx))
    w_ada, w_in, w_out, w_ffn_in, w_ffn_out, w_pw, w_pool = map(f, (w_ada, w_in, w_out, w_ffn_in, w_ffn_out, w_pw, w_pool))
    rpb = f(rpb)
    b_adaT = np.ascontiguousarray(np.stack([_fm(b_ada[l], 96) for l in range(NL)], axis=1))
    gains = np.ascontiguousarray(np.stack(
        [np.stack([_fm(g[l], KC) for g in (g_pre_mix, g_post_mix, g_pre_ffn, g_post_ffn)], axis=1) for l in range(NL)], axis=1))
    wdw = np.ascontiguousarray(np.stack(
        [np.asarray(w_dw, np.float32)[l, :, 0, :].T.reshape(4, 128, 31).transpose(1, 0, 2) for l in range(NL)], axis=1))
    cpar = np.zeros((128, NL, 6, 4), np.float32)
    for l in range(NL):
        for i, a in enumerate((b_dw, ln_conv_g, ln_conv_b, b_pw, pool_scale)):
            cpar[:, l, i, :] = _fm(np.asarray(a)[l], 4)
    ident = np.eye(128, dtype=np.float32)
    ab_s = np.ascontiguousarray(np.stack([_bias_tables(rpb[l], True) for l in range(NL)]))
    ab_p1 = _bias_tables(rpb[0], False)
    ab_p = np.ascontiguousarray(np.broadcast_to(ab_p1[None], (NL,) + ab_p1.shape))
    ic_s, ic_p = _invcnt(True), _invcnt(False)
    zk = np.zeros((NL, NH, 128, 256), np.float32)
    zv = np.zeros((NL, NH, 128, 2, 128), np.float32)

    in_maps = []
    for core in range(8):
        if core < 4:
            xs = x_sample[core]
            cv = c[core]
            kct = np.ascontiguousarray(cache_k[core].transpose(0, 2, 3, 1))
            vct = np.ascontiguousarray(cache_v[core].reshape(NL, 2, 128, NH, 128).transpose(0, 3, 2, 1, 4))
            abt, ict, cbv, flg = ab_s, ic_s, 0.0, 1.0
        else:
            xs = x_prompt[(core - 4) * 8: (core - 4) * 8 + 8].reshape(T, D)
            cv = c_ctx
            kct, vct = zk, zv
            abt, ict, cbv, flg = ab_p, ic_p, NEGM, 0.0
        in_maps.append({
            "xT_in": np.ascontiguousarray(xs.T).reshape(KC, 128, T),
            "cvec": _fm(cv, KC),
            "w_ada": w_ada, "b_adaT": b_adaT, "gains": gains, "w_in": w_in, "w_out": w_out,
            "w_ffn_in": w_ffn_in, "w_ffn_out": w_ffn_out, "w_pw": w_pw, "w_pool": w_pool,
            "wdw": wdw, "cpar": cpar, "ident": ident, "abias": abt, "kctxT": kct, "vctx": vct,
            "cb": np.full((128, 1), cbv, np.float32), "flag": np.full((128, 1), flg, np.float32),
            "invcnt": ict,
        })
    nl = _NC_CACHE.get("n_layers", NL)
    if ("nc", nl) not in _NC_CACHE:
        _NC_CACHE[("nc", nl)] = build_program(nl)
    res = run_bass_kernel_spmd(_NC_CACHE[("nc", nl)], in_maps, core_ids=list(range(8)))
    outs = res.results
    _NC_CACHE["last_res"] = res
    y_prompt = np.zeros((32, 256, D), np.float32)
    y_sample = np.zeros((4, T, D), np.float32)
    nk = np.zeros((32, NL, 256, NH, 128), np.float32)
    nv = np.zeros((32, NL, 256, NH, 128), np.float32)
    for core in range(8):
        y = np.asarray(outs[core]["yT"]).reshape(D, T).T
        if core < 4:
            y_sample[core] = y
        else:
            b0 = (core - 4) * 8
            y_prompt[b0: b0 + 8] = y.reshape(8, 256, D)
            kk = np.asarray(outs[core]["kT_out"]).reshape(NL, NH, 128, 8, 256).transpose(3, 0, 4, 1, 2)
            vv = np.asarray(outs[core]["vT_out"]).reshape(NL, NH, 128, 8, 256).transpose(3, 0, 4, 1, 2)
            nk[b0: b0 + 8] = kk
            nv[b0: b0 + 8] = vv
    return (y_prompt, y_sample, nk, nv)
```

```python
import contextlib
import numpy as np
import concourse.bass as bass
import concourse.mybir as mybir
from concourse.bass_utils import run_bass_kernel_spmd

F32 = mybir.dt.float32
BF16 = mybir.dt.bfloat16
AF = mybir.ActivationFunctionType
ALU = mybir.AluOpType

D = 2048
T = 2048
KC = 16
NL = 4
DIN = 4608
DFF = 5632
KCF = 44
NH = 8
EPS = 1e-6
NEGM = -30000.0
NPAT = 6
ARENA_BYTES = 100 * 1024
ENGS = ["tensor", "vector", "scalar", "gpsimd", "sync"]


def pat_of(i):
    if i == 0:
        return 0
    if i == 1:
        return 1
    if i == 14:
        return 4
    if i == 15:
        return 5
    return 2 if i % 2 == 0 else 3


def wb0_of(i):
    return min(max(i - 2, 0), 11)


class Op:
    __slots__ = ("eng", "fn", "deps", "idx", "needs_inc", "dma_key", "inc_val")

    def __init__(self, eng, fn, deps, idx, dma_key):
        self.eng = eng
        self.fn = fn
        self.deps = deps
        self.idx = idx
        self.needs_inc = False
        self.dma_key = dma_key
        self.inc_val = 0


class Prog:
    def __init__(self):
        self.ops = {e: [] for e in ENGS}
        self.res = {}
        self.dma_cnt = {}

    def _res(self, name):
        r = self.res.get(name)
        if r is None:
            r = {"w": {}, "r": {}}
            self.res[name] = r
        return r

    @staticmethod
    def _merge(dst, src):
        for k, v in src.items():
            if dst.get(k, -1) < v:
                dst[k] = v

    def op(self, eng, fn, r=(), w=(), dma_key=None):
        r = list(r)
        w = list(w)
        if any(n.startswith("@") for n in r + w):
            r.append("ARENA")
        deps = {}
        for n in r:
            self._merge(deps, self._res(n)["w"])
        for n in w:
            rr = self._res(n)
            self._merge(deps, rr["w"])
            self._merge(deps, rr["r"])
        idx = len(self.ops[eng])
        if eng == "tensor":
            deps.pop(("c", "tensor"), None)
        o = Op(eng, fn, deps, idx, dma_key)
        self.ops[eng].append(o)
        for k, v in deps.items():
            if k[0] == "c":
                self.ops[k[1]][v].needs_inc = True
        if dma_key is not None:
            self.dma_cnt[dma_key] = self.dma_cnt.get(dma_key, 0) + 16
            tok = {("d", dma_key): self.dma_cnt[dma_key]}
        else:
            tok = {("c", eng): idx}
        for n in r:
            self._merge(self._res(n)["r"], tok)
        for n in w:
            rr = self._res(n)
            rr["w"] = dict(tok)
            rr["r"] = {}
        return o

    def dma(self, queue, out, in_, key, r=(), w=()):
        def fn(e, out=out, in_=in_):
            return e.dma_start(out=out, in_=in_)
        return self.op(queue, fn, r, w, dma_key=key)

    def emit(self, nc, es):
        csem = {e: es.enter_context(nc.semaphore("c_" + e)) for e in ENGS}
        dsem = {k: es.enter_context(nc.semaphore("d_%d" % i)) for i, k in enumerate(sorted(self.dma_cnt))}
        for e in ENGS:
            c = 0
            for o in self.ops[e]:
                if o.needs_inc and o.dma_key is None:
                    c += 1
                o.inc_val = c
        block = es.enter_context(nc.Block())
        prog = self

        def run(eng_name, eobj):
            waited = {}
            for o in prog.ops[eng_name]:
                for k, v in o.deps.items():
                    if k[0] == "c":
                        sem = csem[k[1]]
                        val = prog.ops[k[1]][v].inc_val
                    else:
                        sem = dsem[k[1]]
                        val = v
                    if waited.get(k, 0) < val:
                        eobj.wait_ge(sem, val)
                        waited[k] = val
                ins = o.fn(eobj)
                if o.dma_key is not None:
                    ins.then_inc(dsem[o.dma_key], 16)
                elif o.needs_inc:
                    ins.then_inc(csem[eng_name], 1)
            if eng_name == "sync":
                for k, v in prog.dma_cnt.items():
                    eobj.wait_ge(dsem[k], v)

        @block.tensor
        def _(t):
            run("tensor", t)

        @block.vector
        def _(v):
            run("vector", v)

        @block.scalar
        def _(s):
            run("scalar", s)

        @block.gpsimd
        def _(g):
            run("gpsimd", g)

        @block.sync
        def _(sy):
            run("sync", sy)


def build_program(n_layers=NL):
    nc = bass.Bass("TRN2", target_bir_lowering=False)

    def din(name, shape):
        return nc.dram_tensor(name, list(shape), F32, kind="ExternalInput").ap()

    xT_in = din("xT_in", [KC, 128, T])
    cvec = din("cvec", [128, KC])
    w_ada = din("w_ada", [NL, D, 6 * D])
    b_adaT = din("b_adaT", [128, NL, 96])
    gains = din("gains", [128, NL, 4, KC])
    w_in = din("w_in", [NL, D, DIN])
    w_out = din("w_out", [NL, D, D])
    w_ffn_in = din("w_ffn_in", [NL, D, 2 * DFF])
    w_ffn_out = din("w_ffn_out", [NL, DFF, D])
    w_pw = din("w_pw", [NL, 512, 512])
    w_pool = din("w_pool", [NL, 4, 128, 128])
    wdw = din("wdw", [128, NL, 4, 31])
    cpar = din("cpar", [128, NL, 6, 4])
    ident_in = din("ident", [128, 128])
    abias = din("abias", [NL, NH, 128, NPAT, 640])
    kctxT = din("kctxT", [NL, NH, 128, 256])
    vctx = din("vctx", [NL, NH, 128, 2, 128])
    cb_in = din("cb", [128, 1])
    flag_in = din("flag", [128, 1])
    invcnt = din("invcnt", [4, 128, T])

    yT = nc.dram_tensor("yT", [KC, 128, T], F32, kind="ExternalOutput").ap()
    kT_out = nc.dram_tensor("kT_out", [NL, NH, 128, T], F32, kind="ExternalOutput").ap()
    vT_out = nc.dram_tensor("vT_out", [NL, NH, 128, T], F32, kind="ExternalOutput").ap()
    sT_d = nc.dram_tensor("sT_d", [KC, 128, T], F32, kind="Internal").ap()
    cat_d = nc.dram_tensor("cat_d", [KC, 128, T], BF16, kind="Internal").ap()

    P = Prog()
    es = contextlib.ExitStack()

    def sb(name, shape, dt):
        return es.enter_context(nc.sbuf_tensor(name, list(shape), dt))

    hT = sb("hT", [128, KC, T], BF16)
    wsl = [sb("wsl%d" % i, [128, 4096], BF16) for i in range(4)]
    arena = sb("arena", [128, ARENA_BYTES // 2], BF16)
    ones_bf = sb("ones_bf", [128, 128], BF16)
    ones_f = sb("ones_f", [128, 128], F32)
    ident = sb("ident_bf", [128, 128], BF16)
    modT = sb("modT", [128, 96], F32)
    vec = sb("vec", [128, 2, 6, KC], F32)
    gains_sb = sb("gains_sb", [128, NL, 4, KC], F32)
    badaT_sb = sb("badaT_sb", [128, 96], F32)
    cvec_sb = sb("cvec_sb", [128, KC], F32)
    siluc = sb("siluc", [128, KC], F32)
    siluc_rep = sb("siluc_rep", [128, KC, 128], BF16)
    mtmp = sb("mtmp", [128, 2, 128], F32)
    cbt = sb("cbt", [128, 256], BF16)
    cpar_sb = sb("cpar_sb", [128, NL, 6, 4], F32)
    cb_sb = sb("cb_sb", [128, 1], F32)
    flag_sb = sb("flag_sb", [128, 1], F32)
    eps_sb = sb("eps_sb", [128, 1], F32)
    dummy = sb("dummy_t", [128, 8], F32)
    psb = [es.enter_context(nc.psum_tensor("psb%d" % i, [128, 512], F32)) for i in range(8)]

    def carve(off, shape, dt):
        n = int(np.prod(shape[1:]))
        nb = n * (4 if dt == F32 else 2)
        assert off % 4 == 0 and off + nb <= ARENA_BYTES, (off, nb)
        a = arena[:, off // 2: (off + nb) // 2]
        if dt == F32:
            a = a.bitcast(F32)
        if len(shape) == 3:
            a = a.rearrange("p (a b) -> p a b", a=shape[1])
        elif len(shape) == 4:
            a = a.rearrange("p (a b c) -> p a b c", a=shape[1], b=shape[2])
        return a

    def barrier():
        P.op("vector", lambda v: v.memset(dummy[:], 0.0), r=[], w=["ARENA"])

    def PE(mms, r, w):
        def fn(t, mms=mms):
            last = None
            for (o, l, rh, st, sp) in mms:
                last = t.matmul(o, lhsT=l, rhs=rh, start=st, stop=sp)
            return last
        return P.op("tensor", fn, r, w)

    def ACT(out, in_, func, r, w, bias=None, scale=None):
        def fn(s, out=out, in_=in_, func=func, bias=bias, scale=scale):
            kw = {}
            if bias is not None:
                kw["bias"] = bias
            if scale is not None:
                kw["scale"] = scale
            return s.activation(out=out, in_=in_, func=func, **kw)
        return P.op("scalar", fn, r, w)

    def DVE(fn, r, w):
        return P.op("vector", fn, r, w)

    wstate = {"n": 0}

    def load_w(src2d, kcn, ncols, rows0=0):
        s = wstate["n"] % 4
        wstate["n"] += 1
        view = wsl[s][:, 0: kcn * ncols].rearrange("p (k n) -> p k n", k=kcn)
        src = src2d[rows0: rows0 + kcn * 128, :].rearrange("(kc p) n -> p kc n", p=128)
        P.dma("gpsimd", view, src, "w%d" % s, r=[], w=["w%d" % s])
        return view, "w%d" % s

    pair_state = {"n": 0}

    PAIRS3 = [(0, 1), (2, 3), (6, 7)]

    def next_pair(npairs=2):
        pair_state["n"] += 1
        if npairs == 3:
            return PAIRS3[pair_state["n"] % 3]
        p = pair_state["n"] % 2
        return (2 * p, 2 * p + 1)

    def mm_half(wv, wres, n_off, kcn, rhs_of, act_res, pair, first=True, last=True):
        mms = []
        for kc in range(kcn):
            for tt in range(2):
                mms.append((psb[pair[tt]][:], wv[:, kc, n_off: n_off + 128], rhs_of(kc, tt),
                            first and kc == 0, last and kc == kcn - 1))
        PE(mms, r=[wres] + list(act_res), w=["ps%d" % pair[0], "ps%d" % pair[1]])

    def act_rhs(half):
        def f(kc, tt, half=half):
            t0 = half * 1024 + tt * 512
            return hT[:, kc, t0: t0 + 512]
        return f

    P.dma("sync", gains_sb[:], gains, "ld_gains", w=["gains"])
    P.dma("sync", cvec_sb[:], cvec, "ld_cvec", w=["cvec"])
    P.dma("sync", cpar_sb[:], cpar, "ld_cpar", w=["cpar"])
    P.dma("sync", cb_sb[:], cb_in, "ld_cb", w=["cb"])
    P.dma("sync", flag_sb[:], flag_in, "ld_flag", w=["flag"])
    P.dma("gpsimd", ident[:], ident_in, "ld_ident", w=["ident"])
    DVE(lambda v: v.memset(ones_bf[:], 1.0), r=[], w=["ones_bf"])
    DVE(lambda v: v.memset(ones_f[:], 1.0), r=[], w=["ones_f"])
    DVE(lambda v: v.memset(eps_sb[:], EPS), r=[], w=["eps"])
    ACT(siluc[:], cvec_sb[:], AF.Silu, r=["cvec"], w=["siluc"])
    DVE(lambda v: v.memset(cbt[:], 1.0), r=[], w=["cbt"])
    DVE(lambda v: v.tensor_scalar(out=cbt[:], in0=cbt[:], scalar1=cb_sb[:], scalar2=None, op0=ALU.mult),
        r=["cbt", "cb"], w=["cbt"])
    DVE(lambda v: v.tensor_copy(out=siluc_rep[:], in_=siluc[:].unsqueeze(2).to_broadcast([128, KC, 128])),
        r=["siluc"], w=["siluc_rep"])

    def modulation(l):
        for wb in range(48):
            wv, wres = load_w(w_ada[l][:, wb * 256: (wb + 1) * 256], KC, 256)
            PE([(psb[7][:, 0:256], siluc_rep[:, kc, :], wv[:, kc, :], kc == 0, kc == KC - 1) for kc in range(KC)],
               r=[wres, "siluc_rep"], w=["ps7"])
            DVE(lambda v: v.tensor_tensor(out=mtmp[:], in0=psb[7][:, 0:256].rearrange("p (a b) -> p a b", a=2),
                                          in1=ident[:].unsqueeze(1).to_broadcast([128, 2, 128]), op=ALU.mult),
                r=["ps7", "ident"], w=["mtmp"])
            DVE(lambda v, wb=wb: v.tensor_reduce(out=modT[:, 2 * wb: 2 * wb + 2], in_=mtmp[:], axis=mybir.AxisListType.X,
                                                 op=ALU.add), r=["mtmp"], w=["modT"])
            yield
        P.dma("sync", badaT_sb[:], b_adaT[:, l, :], "ld_bada", w=["bada"])
        DVE(lambda v, l=l: v.tensor_tensor(out=modT[:], in0=modT[:], in1=badaT_sb[:], op=ALU.add),
            r=["bada", "modT"], w=["modT"])
        s = l % 2
        vr = "vec%d" % s
        DVE(lambda v, l=l, s=s: v.scalar_tensor_tensor(out=vec[:, s, 0, :], in0=modT[:, 16:32], scalar=1.0,
                                                       in1=gains_sb[:, l, 0, :], op0=ALU.add, op1=ALU.mult),
            r=["modT", "gains"], w=[vr])
        DVE(lambda v, s=s: v.tensor_copy(out=vec[:, s, 1, :], in_=modT[:, 0:16]), r=["modT"], w=[vr])
        DVE(lambda v, l=l, s=s: v.tensor_tensor(out=vec[:, s, 2, :], in0=modT[:, 32:48], in1=gains_sb[:, l, 1, :],
                                                op=ALU.mult), r=["modT", "gains"], w=[vr])
        DVE(lambda v, l=l, s=s: v.scalar_tensor_tensor(out=vec[:, s, 3, :], in0=modT[:, 64:80], scalar=1.0,
                                                       in1=gains_sb[:, l, 2, :], op0=ALU.add, op1=ALU.mult),
            r=["modT", "gains"], w=[vr])
        DVE(lambda v, s=s: v.tensor_copy(out=vec[:, s, 4, :], in_=modT[:, 48:64]), r=["modT"], w=[vr])
        DVE(lambda v, l=l, s=s: v.tensor_tensor(out=vec[:, s, 5, :], in0=modT[:, 80:96], in1=gains_sb[:, l, 3, :],
                                                op=ALU.mult), r=["modT", "gains"], w=[vr])

    def xres(oc, half):
        return "xd%d_%d" % (oc, half)

    def prenorm(l, sub, tts, xsrc):
        s = l % 2
        vr = "vec%d" % s
        ia, ib = (0, 1) if sub == 0 else (3, 4)
        xt = [carve(0, [128, KC, 512], F32), carve(32768, [128, KC, 512], F32)]
        sq = carve(65536, [128, KC, 512], BF16)
        rs = [carve(81920, [128, 512], F32), carve(83968, [128, 512], F32)]
        rstd = [carve(86016, [128, 512], F32), carve(88064, [128, 512], F32)]

        def stage_a(n):
            tt = tts[n]
            k = n % 2
            half = tt // 2
            xr = "@xt%d" % k
            P.dma("sync", xt[k], xsrc[:, :, tt * 512: (tt + 1) * 512].rearrange("k p n -> p k n"), "xt%d" % k,
                  r=[xres(oc, half) for oc in range(KC)], w=[xr])
            ACT(sq[:, 0:8, :], xt[k][:, 0:8, :], AF.Square, r=[xr], w=["@sqa"])
            P.op("gpsimd", lambda g, k=k: g.tensor_tensor(out=sq[:, 8:16, :], in0=xt[k][:, 8:16, :], in1=xt[k][:, 8:16, :],
                                                         op=ALU.mult), r=[xr], w=["@sqb"])
            bank = 6 + k
            PE([(psb[bank][:], ones_bf[:], sq[:, kc, :], kc == 0, kc == KC - 1) for kc in range(KC)],
               r=["@sqa", "@sqb", "ones_bf"], w=["ps%d" % bank])
            ACT(rs[k], psb[bank][:], AF.Sqrt, r=["ps%d" % bank, "eps"], w=["@rs%d" % k], bias=eps_sb[:], scale=1.0 / D)

        def stage_b(n):
            k = n % 2
            xr = "@xt%d" % k
            DVE(lambda v, k=k: v.reciprocal(out=rstd[k], in_=rs[k]), r=["@rs%d" % k], w=["@rstd%d" % k])
            DVE(lambda v, k=k: v.tensor_tensor(out=xt[k], in0=xt[k],
                                               in1=rstd[k].unsqueeze(1).to_broadcast([128, KC, 512]), op=ALU.mult),
                r=[xr, "@rstd%d" % k], w=[xr])

        def stage_c(n):
            tt = tts[n]
            k = n % 2
            half = tt // 2
            xr = "@xt%d" % k
            for kc in range(KC):
                ACT(hT[:, kc, tt * 512: (tt + 1) * 512], xt[k][:, kc, :], AF.Identity, r=[xr, vr],
                    w=["act%d" % half], bias=vec[:, s, ib, kc: kc + 1], scale=vec[:, s, ia, kc: kc + 1])

        nt = len(tts)
        stage_a(0)
        for n in range(nt):
            if n + 1 < nt:
                stage_a(n + 1)
            stage_b(n)
            stage_c(n)

    def linear_post(l, sub, half, kcn_total, wsrc, rhs_of, act_res, xsrc):
        s = l % 2
        vr = "vec%d" % s
        ig = 2 if sub == 0 else 5
        base = 90112
        stg = [carve(base, [128, 1024], F32), carve(base + 4096, [128, 1024], F32)]
        sqo = carve(base + 8192, [128, 1024], BF16)
        b6, b7 = 4, 5
        t0 = half * 1024
        keep = (sub == 0)
        obuf = carve(0, [128, KC, 1024], F32) if keep else None
        cur = None
        for oc in range(KC):
            pair = next_pair(3)
            c0 = (oc // 2) * 256
            if kcn_total == KC:
                if oc % 2 == 0:
                    cur = [load_w(wsrc[:, c0: c0 + 256], KC, 256) + (0, KC)]
            else:
                if oc % 2 == 0:
                    cur = [load_w(wsrc[:, c0: c0 + 256], kn, 256, rows0=r0) + (r0 // 128, kn)
                           for (r0, kn) in ((0, 16), (2048, 16), (4096, 12))]
            for g, (wv, wres, kc0, kn) in enumerate(cur):
                mm_half(wv, wres, (oc % 2) * 128, kn,
                        (lambda kc, tt, kc0=kc0: rhs_of(kc0 + kc, tt)), act_res, pair,
                        first=(g == 0), last=(g == len(cur) - 1))
            k = oc % 2
            if keep:
                dst, sr = obuf[:, oc, :], "@ob%d" % oc
            else:
                dst, sr = stg[k], "@stg%d" % k
            for tt in range(2):
                ACT(dst[:, tt * 512: (tt + 1) * 512], psb[pair[tt]][:], AF.Identity,
                    r=["ps%d" % pair[tt]], w=[sr])
            if not keep:
                P.dma("sync", sT_d[oc][:, t0: t0 + 1024], stg[k], "stg%d" % k, r=[sr], w=["sd%d_%d" % (oc, half)])
            ACT(sqo, dst, AF.Square, r=[sr], w=["@sqo"])
            PE([(psb[b6 + tt][:], ones_bf[:], sqo[:, tt * 512: (tt + 1) * 512], oc == 0, oc == KC - 1)
                for tt in range(2)], r=["@sqo", "ones_bf"], w=["ps%d" % b6, "ps%d" % b7])
        if keep:
            rso, rsr = carve(81920 + half * 4096, [128, 1024], F32), "@rsp%d" % half
        else:
            rso, rsr = carve(base + 8192, [128, 1024], F32), "@sqo"
        for tt in range(2):
            ACT(rso[:, tt * 512: (tt + 1) * 512], psb[b6 + tt][:], AF.Sqrt, r=["ps%d" % (b6 + tt), rsr],
                w=[rsr], bias=eps_sb[:], scale=1.0 / D)
        DVE(lambda v: v.reciprocal(out=rso, in_=rso), r=[rsr], w=[rsr])
        if not keep:
            barrier()
        if keep:
            xb = [carve(65536, [128, 2, 1024], F32), carve(73728, [128, 2, 1024], F32), carve(90112, [128, 2, 1024], F32)]
            fb = None
        else:
            xb = [carve(32768 + i * 8192, [128, 2, 1024], F32) for i in range(4)]
            fb = [carve(i * 8192, [128, 2, 1024], F32) for i in range(4)]
        nb = len(xb)

        def loads(g):
            k = g % nb
            P.dma("sync", xb[k], xsrc[2 * g: 2 * g + 2, :, t0: t0 + 1024].rearrange("k p n -> p k n"), "xb%d" % k,
                  r=[xres(2 * g, half), xres(2 * g + 1, half)], w=["@xb%d" % k])
            if not keep:
                P.dma("sync", fb[k], sT_d[2 * g: 2 * g + 2, :, t0: t0 + 1024].rearrange("k p n -> p k n"), "fb%d" % k,
                      r=["sd%d_%d" % (2 * g, half), "sd%d_%d" % (2 * g + 1, half)], w=["@fb%d_0" % k, "@fb%d_1" % k])

        for g in range(nb - 1):
            loads(g)
        for g in range(8):
            k = g % nb
            if g + nb - 1 < 8:
                loads(g + nb - 1)
            if keep:
                src = obuf[:, 2 * g: 2 * g + 2, :]
                srs = ["@ob%d" % (2 * g), "@ob%d" % (2 * g + 1)]
            else:
                src = fb[k]
                srs = ["@fb%d_0" % k, "@fb%d_1" % k]
            for n in range(2):
                oc = 2 * g + n
                DVE(lambda v, src=src, n=n, oc=oc: v.scalar_tensor_tensor(
                    out=src[:, n, :], in0=src[:, n, :], scalar=vec[:, s, ig, oc: oc + 1], in1=rso,
                    op0=ALU.mult, op1=ALU.mult), r=[srs[n], rsr, vr], w=[srs[n]])
            if g % 2 == 0:
                P.op("gpsimd", lambda e, k=k, src=src: e.tensor_tensor(out=xb[k], in0=xb[k], in1=src, op=ALU.add),
                     r=srs + ["@xb%d" % k], w=["@xb%d" % k])
            else:
                DVE(lambda v, k=k, src=src: v.tensor_tensor(out=xb[k], in0=xb[k], in1=src, op=ALU.add),
                    r=srs + ["@xb%d" % k], w=["@xb%d" % k])
            P.dma("sync" if keep else "scalar", yT[2 * g: 2 * g + 2, :, t0: t0 + 1024].rearrange("k p n -> p k n"), xb[k], "xbs%d" % k,
                  r=["@xb%d" % k], w=[xres(2 * g, half), xres(2 * g + 1, half)])

    def attention(l, modgen):
        KT = carve(0, [128, 4, T], BF16)
        QT = carve(16384, [128, 4, T], BF16)
        Vtok = carve(32768, [128, 4, 16, 128], BF16)
        vTb = carve(49152, [128, T], BF16)
        aTs = [carve(53248, [128, T], BF16), carve(57344, [128, T], BF16)]
        stg = [carve(61440, [128, 1024], F32), carve(65536, [128, 1024], F32)]
        ab = [carve(69632, [128, NPAT, 640], BF16), carve(77312, [128, NPAT, 640], BF16)]
        kc_sb = carve(84992, [128, 4, 256], BF16)
        vc_sb = carve(87040, [128, 4, 2, 128], BF16)
        PT = [carve(89088, [128, 896], BF16), carve(90880, [128, 896], BF16)]
        rden = carve(92672, [128, 512], F32)
        scale = 1.0 / np.sqrt(128.0)
        stn = {"n": 0}
        abn = {"n": 0}
        ptn = {"n": 0}
        hcount = 0
        for g in range(2):
            P.dma("gpsimd", kc_sb, kctxT[l, 4 * g: 4 * g + 4].rearrange("h p n -> p h n"), "kcsb", w=["@kcsb"])
            P.dma("gpsimd", vc_sb, vctx[l, 4 * g: 4 * g + 4].rearrange("h p b d -> p h b d"), "vcsb", w=["@vcsb"])
            for which, c0 in (("k", 1024 + 512 * g), ("v", 2048 + 512 * g), ("q", 512 * g)):
                for blk in range(2):
                    wv, wres = load_w(w_in[l][:, c0 + blk * 256: c0 + blk * 256 + 256], KC, 256)
                    for n in range(2):
                        hl = blk * 2 + n
                        h = 4 * g + hl
                        for half in range(2):
                            pair = next_pair()
                            mm_half(wv, wres, n * 128, KC, act_rhs(half), ["act%d" % half], pair)
                            t0 = half * 1024
                            if which == "q":
                                for tt in range(2):
                                    ACT(QT[:, hl, t0 + tt * 512: t0 + tt * 512 + 512], psb[pair[tt]][:], AF.Copy,
                                        r=["ps%d" % pair[tt]], w=["@QT%d" % hl], scale=float(scale))
                            else:
                                k = stn["n"] % 2
                                stn["n"] += 1
                                sr = "@astg%d" % k
                                for tt in range(2):
                                    ACT(stg[k][:, tt * 512: (tt + 1) * 512], psb[pair[tt]][:], AF.Identity,
                                        r=["ps%d" % pair[tt]], w=[sr])
                                dst = (kT_out if which == "k" else vT_out)[l, h][:, t0: t0 + 1024]
                                P.dma("sync", dst, stg[k], "astg%d" % k, r=[sr], w=["kvout_%s_%d_%d" % (which, h, half)])
                                if which == "k":
                                    DVE(lambda v, k=k, hl=hl, t0=t0: v.tensor_copy(out=KT[:, hl, t0: t0 + 1024], in_=stg[k]),
                                        r=[sr], w=["@KT%d" % hl])
                                else:
                                    DVE(lambda v, k=k, t0=t0: v.tensor_copy(out=vTb[:, t0: t0 + 1024], in_=stg[k]),
                                        r=[sr], w=["@vTb"])
                                    pb = psb[6 + half]
                                    pbv = pb[:].bitcast(BF16)
                                    def tfn(t, t0=t0, pbv=pbv):
                                        last = None
                                        for b in range(8):
                                            last = t.transpose(pbv[:, b * 128: (b + 1) * 128],
                                                               vTb[:, t0 + b * 128: t0 + (b + 1) * 128], ident[:])
                                        return last
                                    P.op("tensor", tfn, r=["@vTb", "ident"], w=["ps%d" % (6 + half)])
                                    DVE(lambda v, hl=hl, half=half, pbv=pbv: v.tensor_copy(
                                        out=Vtok[:, hl, half * 8: half * 8 + 8, :].rearrange("p a b -> p (a b)"), in_=pbv),
                                        r=["ps%d" % (6 + half)], w=["@Vtok%d" % hl])
            for hl in range(4):
                h = 4 * g + hl
                ak = abn["n"] % 2
                abn["n"] += 1
                abr = "@ab%d" % ak
                P.dma("gpsimd", ab[ak], abias[l, h], "ab%d" % ak, w=[abr])
                aT = aTs[hcount % 2]
                aTr = "@aT%d" % (hcount % 2)
                hcount += 1
                def s_stage(i, hl=hl, ak=ak, abr=abr):
                    pid = pat_of(i)
                    wb0 = wb0_of(i)
                    pair = (0, 1) if i % 2 == 0 else (2, 3)
                    bA, bB = psb[pair[0]], psb[pair[1]]
                    q_ap = QT[:, hl, i * 128: (i + 1) * 128]
                    mms = [(bA[:, 0:512], ident[:], ab[ak][:, pid, 0:512], True, False)]
                    for j in range(4):
                        kb = wb0 + j
                        mms.append((bA[:, j * 128: (j + 1) * 128], KT[:, hl, kb * 128: (kb + 1) * 128], q_ap,
                                    False, j == 3))
                    mms.append((bB[:, 0:128], ident[:], ab[ak][:, pid, 512:640], True, False))
                    mms.append((bB[:, 128:384], ident[:], cbt[:], False, False))
                    kb = wb0 + 4
                    mms.append((bB[:, 0:128], KT[:, hl, kb * 128: (kb + 1) * 128], q_ap, False, False))
                    for cbk in range(2):
                        mms.append((bB[:, 128 + cbk * 128: 256 + cbk * 128], kc_sb[:, hl, cbk * 128: (cbk + 1) * 128],
                                    q_ap, False, cbk == 1))
                    PE(mms, r=[abr, "ident", "cbt", "@KT%d" % hl, "@QT%d" % hl, "@kcsb"],
                       w=["ps%d" % pair[0], "ps%d" % pair[1]])
                    pk = i % 2
                    ACT(PT[pk][:, 0:512], bA[:], AF.Exp, r=["ps%d" % pair[0]], w=["@PTa%d" % pk])
                    ACT(PT[pk][:, 512:896], bB[:, 0:384], AF.Exp, r=["ps%d" % pair[1]], w=["@PTb%d" % pk])

                def o_stage(i, hl=hl, aT=aT, aTr=aTr):
                    qg, ii = i // 4, i % 4
                    wb0 = wb0_of(i)
                    ob, db = (4, 5) if qg % 2 == 0 else (6, 7)
                    pk = i % 2
                    mms = []
                    for j in range(7):
                        mms.append((psb[db][:, ii * 128: (ii + 1) * 128], ones_bf[:], PT[pk][:, j * 128: (j + 1) * 128],
                                    j == 0, j == 6))
                    for j in range(7):
                        if j < 5:
                            vl = Vtok[:, hl, wb0 + j, :]
                        else:
                            vl = vc_sb[:, hl, j - 5, :]
                        mms.append((psb[ob][:, ii * 128: (ii + 1) * 128], vl, PT[pk][:, j * 128: (j + 1) * 128],
                                    j == 0, j == 6))
                    PE(mms, r=["@PTa%d" % pk, "@PTb%d" % pk, "ones_bf", "@Vtok%d" % hl, "@vcsb"],
                       w=["ps%d" % ob, "ps%d" % db])
                    if ii == 3:
                        DVE(lambda v, db=db: v.reciprocal(out=rden, in_=psb[db][:]), r=["ps%d" % db], w=["@rden"])
                        DVE(lambda v, ob=ob, qg=qg, aT=aT: v.tensor_tensor(out=aT[:, qg * 512: (qg + 1) * 512],
                                                                          in0=psb[ob][:], in1=rden, op=ALU.mult),
                            r=["ps%d" % ob, "@rden"], w=[aTr])

                s_stage(0)
                for i in range(16):
                    if i + 1 < 16:
                        s_stage(i + 1)
                    o_stage(i)
                P.dma("sync", cat_d[h], aT, "aT%d" % ((hcount - 1) % 2), r=[aTr], w=["cat%d" % h])
                if modgen is not None:
                    for _ in range(6):
                        next(modgen, None)

    def conv_module(l):
        sig = carve(0, [128, 2, T], F32)
        hbufs = [carve(16384, [128, 8, 286], BF16), carve(20992, [128, 8, 286], BF16)]
        dgs = [carve(25600, [128, 31, 128], BF16), carve(33536, [128, 31, 128], BF16)]
        cv = carve(41472, [128, 4, T], F32)
        hsil = carve(74240, [128, 4, T], BF16)
        stg = [carve(90624, [128, 1024], BF16), carve(92672, [128, 1024], BF16)]
        wdw_l = carve(94720, [128, 4, 31], F32)
        P.dma("sync", wdw_l, wdw[:, l], "ld_wdw", w=["@wdw"])
        for pr in range(2):
            wv, wres = load_w(w_in[l][:, 3584 + pr * 256: 3584 + pr * 256 + 256], KC, 256)
            for n in range(2):
                for half in range(2):
                    pair = next_pair()
                    mm_half(wv, wres, n * 128, KC, act_rhs(half), ["act%d" % half], pair)
                    for tt in range(2):
                        t0 = half * 1024 + tt * 512
                        ACT(sig[:, n, t0: t0 + 512], psb[pair[tt]][:], AF.Sigmoid, r=["ps%d" % pair[tt]], w=["@sig%d" % n])
            wv, wres = load_w(w_in[l][:, 3072 + pr * 256: 3072 + pr * 256 + 256], KC, 256)
            for n in range(2):
                hbuf, hr = hbufs[n], "@hbuf%d" % n
                DVE(lambda v, hbuf=hbuf: v.memset(hbuf, 0.0), r=[], w=[hr])
                for half in range(2):
                    pair = next_pair()
                    mm_half(wv, wres, n * 128, KC, act_rhs(half), ["act%d" % half], pair)
                    for tt in range(2):
                        sg0 = half * 4 + tt * 2
                        t0 = half * 1024 + tt * 512
                        DVE(lambda v, n=n, t0=t0, sg0=sg0, pb=psb[pair[tt]], hbuf=hbuf: v.tensor_tensor(
                            out=hbuf[:, sg0: sg0 + 2, 15: 271],
                            in0=pb[:].rearrange("p (a b) -> p a b", a=2),
                            in1=sig[:, n, t0: t0 + 512].rearrange("p (a b) -> p a b", a=2), op=ALU.mult),
                            r=["ps%d" % pair[tt], "@sig%d" % n], w=[hr])
            for n in range(2):
                cc = 2 * pr + n
                hbuf, hr = hbufs[n], "@hbuf%d" % n
                dg, dr = dgs[n], "@dg%d" % n
                DVE(lambda v, hbuf=hbuf: v.tensor_scalar(out=hbuf[:, 1:8, 0:15], in0=hbuf[:, 0:7, 256:271], scalar1=flag_sb[:],
                                                         scalar2=None, op0=ALU.mult), r=[hr, "flag"], w=[hr])
                DVE(lambda v, hbuf=hbuf: v.tensor_scalar(out=hbuf[:, 0:7, 271:286], in0=hbuf[:, 1:8, 15:30], scalar1=flag_sb[:],
                                                         scalar2=None, op0=ALU.mult), r=[hr, "flag"], w=[hr])
                DVE(lambda v, cc=cc, dg=dg: v.tensor_tensor(
                    out=dg, in0=ident[:].unsqueeze(1).to_broadcast([128, 31, 128]),
                    in1=wdw_l[:, cc, :].unsqueeze(2).to_broadcast([128, 31, 128]), op=ALU.mult),
                    r=["ident", "@wdw"], w=[dr])
                for half in range(2):
                    pair = next_pair()
                    mms = []
                    for tt in range(2):
                        sg0 = half * 4 + tt * 2
                        for j in range(31):
                            mms.append((psb[pair[tt]][:].rearrange("p (a b) -> p a b", a=2), dg[:, j, :],
                                        hbuf[:, sg0: sg0 + 2, j: j + 256], j == 0, j == 30))
                    PE(mms, r=[dr, hr], w=["ps%d" % pair[0], "ps%d" % pair[1]])
                    for tt in range(2):
                        t0_ = half * 1024 + tt * 512
                        ACT(cv[:, cc, t0_: t0_ + 512], psb[pair[tt]][:], AF.Identity, r=["ps%d" % pair[tt], "cpar"],
                            w=["@cv%d" % cc], bias=cpar_sb[:, l, 0, cc: cc + 1])
        barrier()
        cvr = ["@cv%d" % c for c in range(4)]
        tmpas = [carve(16384, [128, 512], F32), carve(18432, [128, 512], F32)]
        tmpbs = [carve(20480, [128, 512], F32), carve(22528, [128, 512], F32)]
        tmpcs = [carve(24576, [128, 512], F32), carve(26624, [128, 512], F32)]
        sqcs = [carve(28672, [128, 512], F32), carve(30720, [128, 512], F32)]
        ln_banks = [(6, 7), (4, 5)]

        def ln_stats(tt):
            sl = slice(tt * 512, (tt + 1) * 512)
            bm, be = ln_banks[tt % 2]
            PE([(psb[bm][:], ones_f[:], cv[:, cc, sl], cc == 0, cc == 3) for cc in range(4)], r=cvr + ["ones_f"],
               w=["ps%d" % bm])
            for cc in range(4):
                q = sqcs[cc % 2]
                ACT(q, cv[:, cc, sl], AF.Square, r=["@cv%d" % cc], w=["@sqc%d" % (cc % 2)])
                PE([(psb[be][:], ones_f[:], q, cc == 0, cc == 3)], r=["@sqc%d" % (cc % 2), "ones_f"], w=["ps%d" % be])

        def ln_apply(tt):
            sl = slice(tt * 512, (tt + 1) * 512)
            bm, be = ln_banks[tt % 2]
            k = tt % 2
            ta, tb = tmpas[k], tmpbs[k]
            ra, rb = "@tmpa%d" % k, "@tmpb%d" % k
            ACT(ta, psb[bm][:], AF.Copy, r=["ps%d" % bm], w=[ra], scale=1.0 / 512)
            DVE(lambda v: v.tensor_tensor(out=tb, in0=ta, in1=ta, op=ALU.mult), r=[ra], w=[rb])
            DVE(lambda v: v.scalar_tensor_tensor(out=tb, in0=psb[be][:], scalar=1.0 / 512, in1=tb, op0=ALU.mult,
                                                 op1=ALU.subtract), r=["ps%d" % be, rb], w=[rb])
            ACT(tb, tb, AF.Sqrt, r=[rb, "eps"], w=[rb], bias=eps_sb[:], scale=1.0)
            DVE(lambda v: v.reciprocal(out=tb, in_=tb), r=[rb], w=[rb])
            for cc in range(4):
                tc_ = tmpcs[cc % 2]
                rc = "@tmpc%d" % (cc % 2)
                DVE(lambda v, cc=cc, tc_=tc_: v.tensor_tensor(out=tc_, in0=cv[:, cc, sl], in1=ta, op=ALU.subtract),
                    r=["@cv%d" % cc, ra], w=[rc])
                DVE(lambda v, tc_=tc_: v.tensor_tensor(out=tc_, in0=tc_, in1=tb, op=ALU.mult), r=[rc, rb], w=[rc])
                ACT(hsil[:, cc, sl], tc_, AF.Silu, r=[rc, "cpar"], w=["@hsil"],
                    bias=cpar_sb[:, l, 2, cc: cc + 1], scale=cpar_sb[:, l, 1, cc: cc + 1])

        ln_stats(0)
        for tt in range(4):
            if tt + 1 < 4:
                ln_stats(tt + 1)
            ln_apply(tt)
        wv, wres = load_w(w_pw[l], 4, 512)
        n_st = 0
        for co in range(4):
            for half in range(2):
                pair = next_pair()
                mm_half(wv, wres, co * 128, 4, (lambda kc, tt, half=half: hsil[:, kc, half * 1024 + tt * 512: half * 1024 + tt * 512 + 512]),
                        ["@hsil"], pair)
                k = n_st % 2
                n_st += 1
                for tt in range(2):
                    ACT(stg[k][:, tt * 512: (tt + 1) * 512], psb[pair[tt]][:], AF.Identity, r=["ps%d" % pair[tt], "cpar"],
                        w=["@cstg%d" % k], bias=cpar_sb[:, l, 3, co: co + 1])
                P.dma("sync", cat_d[8 + co][:, half * 1024: half * 1024 + 1024], stg[k], "cstg%d" % k,
                      r=["@cstg%d" % k], w=["cat%d" % (8 + co)])

    def pool_mixer(l):
        ubs = [carve(i * 8704, [128, 8, 272], F32) for i in range(4)]
        p2 = carve(34816, [128, 8, 272], F32)
        p4 = carve(43520, [128, 8, 272], F32)
        p8 = carve(52224, [128, 8, 272], F32)
        p16 = carve(60928, [128, 8, 272], F32)
        icns = [carve(69632, [128, 8, 256], F32), carve(77824, [128, 8, 256], F32)]
        dbfs = [carve(86016, [128, T], BF16), carve(90112, [128, T], BF16)]
        stg = [carve(94208, [128, 1024], BF16), carve(96256, [128, 1024], BF16)]
        wpool_sb = carve(98304, [128, 4, 128], BF16)
        P.dma("gpsimd", wpool_sb, w_pool[l].rearrange("g i o -> i g o"), "wpool", w=["@wpool"])
        n_st = 0
        for gi in range(4):
            DVE(lambda v, gi=gi: v.memset(ubs[gi], 0.0), r=[], w=["@ub%d" % gi])
        for gi in range(2):
            P.dma("sync", icns[gi], invcnt[gi].rearrange("p (a b) -> p a b", a=8), "icn%d" % gi, w=["@icn%d" % gi])
        for pr in range(2):
            wv, wres = load_w(w_in[l][:, 4096 + pr * 256: 4096 + pr * 256 + 256], KC, 256)
            for n in range(2):
                gi = 2 * pr + n
                ub = ubs[gi]
                for half in range(2):
                    pair = next_pair()
                    mm_half(wv, wres, n * 128, KC, act_rhs(half), ["act%d" % half], pair)
                    for tt in range(2):
                        sg0 = half * 4 + tt * 2
                        ACT(ub[:, sg0: sg0 + 2, 8: 264], psb[pair[tt]][:].rearrange("p (a b) -> p a b", a=2), AF.Identity,
                            r=["ps%d" % pair[tt]], w=["@ub%d" % gi])
        for gi in range(4):
            ub, ur = ubs[gi], "@ub%d" % gi
            icn, ir = icns[gi % 2], "@icn%d" % (gi % 2)
            dbf, dr = dbfs[gi % 2], "@dbf%d" % (gi % 2)
            lv = [ub, p2, p4, p8, p16]
            DVE(lambda v, ub=ub: v.tensor_scalar(out=ub[:, 1:8, 0:8], in0=ub[:, 0:7, 256:264], scalar1=flag_sb[:],
                                                 scalar2=None, op0=ALU.mult), r=[ur, "flag"], w=[ur])
            DVE(lambda v, ub=ub: v.tensor_scalar(out=ub[:, 0:7, 264:272], in0=ub[:, 1:8, 8:16], scalar1=flag_sb[:],
                                                 scalar2=None, op0=ALU.mult), r=[ur, "flag"], w=[ur])
            DVE(lambda v, ub=ub: v.tensor_tensor(out=p2[:, :, 1:272], in0=ub[:, :, 0:271], in1=ub[:, :, 1:272], op=ALU.add),
                r=[ur], w=["@lv"])
            for k in range(2, gi + 2):
                sh = 1 << (k - 2)
                lo = (1 << (k - 1))
                src, dst = lv[k - 1], lv[k]
                DVE(lambda v, src=src, dst=dst, sh=sh, lo=lo: v.tensor_tensor(
                    out=dst[:, :, lo: 272 - lo], in0=src[:, :, lo - sh: 272 - lo - sh],
                    in1=src[:, :, lo + sh: 272 - lo + sh], op=ALU.add), r=["@lv"], w=["@lv"])
            top = lv[gi + 1]
            DVE(lambda v, top=top, icn=icn: v.tensor_tensor(out=top[:, :, 8:264], in0=top[:, :, 8:264], in1=icn, op=ALU.mult),
                r=["@lv", ir], w=["@lv"])
            DVE(lambda v, top=top, ub=ub, dbf=dbf: v.tensor_tensor(out=dbf.rearrange("p (a b) -> p a b", a=8),
                                                                  in0=top[:, :, 8:264], in1=ub[:, :, 8:264], op=ALU.subtract),
                r=["@lv", ur], w=[dr])
            if gi + 2 < 4:
                P.dma("sync", icns[gi % 2], invcnt[gi + 2].rearrange("p (a b) -> p a b", a=8), "icn%d" % (gi % 2),
                      w=["@icn%d" % (gi % 2)])
            for half in range(2):
                pair = next_pair()
                PE([(psb[pair[tt]][:], wpool_sb[:, gi, :], dbf[:, half * 1024 + tt * 512: half * 1024 + tt * 512 + 512],
                     True, True) for tt in range(2)], r=["@wpool", dr], w=["ps%d" % pair[0], "ps%d" % pair[1]])
                k = n_st % 2
                n_st += 1
                for tt in range(2):
                    ACT(stg[k][:, tt * 512: (tt + 1) * 512], psb[pair[tt]][:], AF.Identity, r=["ps%d" % pair[tt], "cpar"],
                        w=["@pstg%d" % k], scale=cpar_sb[:, l, 4, gi: gi + 1])
                P.dma("sync", cat_d[12 + gi][:, half * 1024: half * 1024 + 1024], stg[k], "pstg%d" % k,
                      r=["@pstg%d" % k], w=["cat%d" % (12 + gi)])

    def ffn_in(l, half, actb):
        sg = [carve(90112, [128, 1024], F32), carve(94208, [128, 1024], F32)]
        n_sg = 0
        for jb in range(22):
            wg, wgr = load_w(w_ffn_in[l][:, jb * 256: jb * 256 + 256], KC, 256)
            wu, wur = load_w(w_ffn_in[l][:, DFF + jb * 256: DFF + jb * 256 + 256], KC, 256)
            for n in range(2):
                j = jb * 2 + n
                pg = next_pair(3)
                mm_half(wg, wgr, n * 128, KC, act_rhs(half), ["act%d" % half], pg)
                pu = next_pair(3)
                mm_half(wu, wur, n * 128, KC, act_rhs(half), ["act%d" % half], pu)
                k = n_sg % 2
                n_sg += 1
                for tt in range(2):
                    ACT(sg[k][:, tt * 512: (tt + 1) * 512], psb[pg[tt]][:], AF.Silu, r=["ps%d" % pg[tt]], w=["@sg%d" % k])
                    DVE(lambda v, k=k, tt=tt, j=j, pb=psb[pu[tt]]: v.tensor_tensor(
                        out=actb[:, j, tt * 512: (tt + 1) * 512], in0=sg[k][:, tt * 512: (tt + 1) * 512], in1=pb[:],
                        op=ALU.mult), r=["@sg%d" % k, "ps%d" % pu[tt]], w=["@actb"])

    for _ in modulation(0):
        pass
    for l in range(n_layers):
        xsrc = xT_in if l == 0 else yT
        barrier()
        prenorm(l, 0, [0, 1, 2, 3], xsrc)
        barrier()
        modgen = modulation(l + 1) if l + 1 < n_layers else None
        attention(l, modgen)
        if modgen is not None:
            for _ in modgen:
                pass
        barrier()
        conv_module(l)
        barrier()
        pool_mixer(l)
        barrier()
        for half in range(2):
            P.dma("sync", hT[:, :, half * 1024: (half + 1) * 1024],
                  cat_d[:, :, half * 1024: (half + 1) * 1024].rearrange("k p n -> p k n"), "catld%d" % half,
                  r=["cat%d" % c for c in range(KC)], w=["act%d" % half])
        for half in range(2):
            linear_post(l, 0, half, KC, w_out[l], act_rhs(half), ["act%d" % half], xsrc)
        barrier()
        prenorm(l, 1, [0, 1, 2, 3], yT)
        for half in range(2):
            barrier()
            actb = carve(0, [128, KCF, 1024], BF16)
            ffn_in(l, half, actb)
            barrier()
            linear_post(l, 1, half, KCF, w_ffn_out[l],
                        (lambda kc, tt, actb=actb: actb[:, kc, tt * 512: (tt + 1) * 512]), ["@actb"], yT)
    P.emit(nc, es)
    es.close()
    return nc


def _fm(v, nch):
    return np.ascontiguousarray(np.asarray(v, np.float32).reshape(nch, 128).T)


def _bias_tables(rpb_l, sample):
    reps = [0, 1, 2, 3, 14, 15]
    out = np.full((NH, 128, NPAT, 5, 128), NEGM, np.float32)
    kk = np.arange(128)
    qq = np.arange(128)
    for pi, i in enumerate(reps):
        wb0 = wb0_of(i)
        for j in range(5):
            kb = wb0 + j
            if sample:
                kr = (2 * kb + kk // 64)[:, None]
                kcol = (kk % 64)[:, None]
                qr = (2 * i + qq // 64)[None, :]
                qcol = (qq % 64)[None, :]
                sr = np.clip(qr - 4, 0, 24)
                qstart = np.clip(qcol - 8, 0, 48)
                valid = (kr >= sr) & (kr < sr + 8) & (kcol >= qstart) & (kcol < qstart + 16)
                ridx = np.clip(kr - qr + 7, 0, 14)
                cidx = np.clip(kcol - qcol, -15, 15) + 15
                vals = rpb_l[:, ridx, cidx]
                out[:, :, pi, j, :] = np.where(valid[None], vals, np.float32(NEGM))
            else:
                if kb // 2 == i // 2:
                    out[:, :, pi, j, :] = 0.0
    return out.reshape(NH, 128, NPAT, 640)


def _invcnt(sample):
    res = np.zeros((4, T), np.float32)
    L = T if sample else 256
    t = np.arange(T) % L
    for gi, w in enumerate((2, 4, 8, 16)):
        lo = np.clip(t - w // 2, 0, L)
        hi = np.clip(t - w // 2 + w, 0, L)
        res[gi] = 1.0 / (hi - lo).astype(np.float32)
    return np.ascontiguousarray(np.broadcast_to(res[:, None, :], (4, 128, T)))


_NC_CACHE = {}


def kernel(x_prompt, x_sample, cache_k, cache_v, c, c_ctx, w_ada, b_ada, g_pre_mix, g_post_mix,
           g_pre_ffn, g_post_ffn, w_in, rpb, w_dw, b_dw, ln_conv_g, ln_conv_b, w_pw, b_pw,
           w_pool, pool_scale, w_out, w_ffn_in, w_ffn_out):
    f = lambda a: np.ascontiguousarray(np.asarray(a, np.float32))
    x_prompt, x_sample, cache_k, cache_v, c, c_ctx = map(f, (x_prompt, x_sample, cache_k, cache_v, c, c_ctx))
    w_ada, w_in, w_out, w_ffn_in, w_ffn_out, w_pw, w_pool = map(f, (w_ada, w_in, w_out, w_ffn_in, w_ffn_out, w_pw, w_pool))
    rpb = f(rpb)
    b_adaT = np.ascontiguousarray(np.stack([_fm(b_ada[l], 96) for l in range(NL)], axis=1))
    gains = np.ascontiguousarray(np.stack(
        [np.stack([_fm(g[l], KC) for g in (g_pre_mix, g_post_mix, g_pre_ffn, g_post_ffn)], axis=1) for l in range(NL)], axis=1))
    wdw = np.ascontiguousarray(np.stack(
        [np.asarray(w_dw, np.float32)[l, :, 0, :].T.reshape(4, 128, 31).transpose(1, 0, 2) for l in range(NL)], axis=1))
    cpar = np.zeros((128, NL, 6, 4), np.float32)
    for l in range(NL):
        for i, a in enumerate((b_dw, ln_conv_g, ln_conv_b, b_pw, pool_scale)):
            cpar[:, l, i, :] = _fm(np.asarray(a)[l], 4)
    ident = np.eye(128, dtype=np.float32)
    ab_s = np.ascontiguousarray(np.stack([_bias_tables(rpb[l], True) for l in range(NL)]))
    ab_p1 = _bias_tables(rpb[0], False)
    ab_p = np.ascontiguousarray(np.broadcast_to(ab_p1[None], (NL,) + ab_p1.shape))
    ic_s, ic_p = _invcnt(True), _invcnt(False)
    zk = np.zeros((NL, NH, 128, 256), np.float32)
    zv = np.zeros((NL, NH, 128, 2, 128), np.float32)

    in_maps = []
    for core in range(8):
        if core < 4:
            xs = x_sample[core]
            cv = c[core]
            kct = np.ascontiguousarray(cache_k[core].transpose(0, 2, 3, 1))
            vct = np.ascontiguousarray(cache_v[core].reshape(NL, 2, 128, NH, 128).transpose(0, 3, 2, 1, 4))
            abt, ict, cbv, flg = ab_s, ic_s, 0.0, 1.0
        else:
            xs = x_prompt[(core - 4) * 8: (core - 4) * 8 + 8].reshape(T, D)
            cv = c_ctx
            kct, vct = zk, zv
            abt, ict, cbv, flg = ab_p, ic_p, NEGM, 0.0
        in_maps.append({
            "xT_in": np.ascontiguousarray(xs.T).reshape(KC, 128, T),
            "cvec": _fm(cv, KC),
            "w_ada": w_ada, "b_adaT": b_adaT, "gains": gains, "w_in": w_in, "w_out": w_out,
            "w_ffn_in": w_ffn_in, "w_ffn_out": w_ffn_out, "w_pw": w_pw, "w_pool": w_pool,
            "wdw": wdw, "cpar": cpar, "ident": ident, "abias": abt, "kctxT": kct, "vctx": vct,
            "cb": np.full((128, 1), cbv, np.float32), "flag": np.full((128, 1), flg, np.float32),
            "invcnt": ict,
        })
    nl = _NC_CACHE.get("n_layers", NL)
    if ("nc", nl) not in _NC_CACHE:
        _NC_CACHE[("nc", nl)] = build_program(nl)
    res = run_bass_kernel_spmd(_NC_CACHE[("nc", nl)], in_maps, core_ids=list(range(8)))
    outs = res.results
    _NC_CACHE["last_res"] = res
    y_prompt = np.zeros((32, 256, D), np.float32)
    y_sample = np.zeros((4, T, D), np.float32)
    nk = np.zeros((32, NL, 256, NH, 128), np.float32)
    nv = np.zeros((32, NL, 256, NH, 128), np.float32)
    for core in range(8):
        y = np.asarray(outs[core]["yT"]).reshape(D, T).T
        if core < 4:
            y_sample[core] = y
        else:
            b0 = (core - 4) * 8
            y_prompt[b0: b0 + 8] = y.reshape(8, 256, D)
            kk = np.asarray(outs[core]["kT_out"]).reshape(NL, NH, 128, 8, 256).transpose(3, 0, 4, 1, 2)
            vv = np.asarray(outs[core]["vT_out"]).reshape(NL, NH, 128, 8, 256).transpose(3, 0, 4, 1, 2)
            nk[b0: b0 + 8] = kk
            nv[b0: b0 + 8] = vv
    return (y_prompt, y_sample, nk, nv)
```
